# Optimizing a Trainium2 kernel written in Bass

```python
import math
import jax
import jax.numpy as jnp
from jax import lax
import numpy as np

D_MODEL = 1024
BATCH = 4
SEQ = 8192
DEPTH = 4
DEC_BATCH = 16
DEC_SEQ = 16
PAST_LEN = 2048

CHUNK = 64
N_MIXERS = 2
N_GDN = (DEPTH + 1) // 2
N_DIFF = DEPTH // 2
RMS_EPS = 1e-6
F32 = jnp.float32

GDN_HEADS = 8
GDN_DK = 128
GDN_DV = 128
GDN_QK_WIDTH = GDN_HEADS * GDN_DK
GDN_WIDTH = GDN_HEADS * GDN_DV
CONV_WIDTH = 4
GDN_CONV_CH = 2 * GDN_QK_WIDTH + GDN_WIDTH
GDN_IN = GDN_CONV_CH + GDN_WIDTH + 2 * GDN_HEADS

DIFF_HEADS = 8
DIFF_QK_DIM = 64
DIFF_V_DIM = 2 * DIFF_QK_DIM
DIFF_QK_WIDTH = DIFF_HEADS * 2 * DIFF_QK_DIM
DIFF_WIDTH = DIFF_HEADS * DIFF_V_DIM
DIFF_IN = 2 * DIFF_QK_WIDTH + 2 * DIFF_WIDTH
Q_BLOCK = 128
ROPE_THETA = 10000.0

kernel_name = "hybrid_gdn_diffattn_stream_step"


def rmsnorm(x, w):
    xf = x.astype(F32)
    y = xf * lax.rsqrt(jnp.mean(xf * xf, axis=-1, keepdims=True) + RMS_EPS)
    return (y * w.astype(F32)).astype(x.dtype)


def l2norm(x):
    return x * lax.rsqrt(jnp.sum(x * x, axis=-1, keepdims=True) + 1e-6)


def rope(x, pos):
    half = x.shape[-1] // 2
    inv = 1.0 / (ROPE_THETA ** (jnp.arange(half, dtype=F32) / half))
    ang = pos.astype(F32)[:, None] * inv[None, :]
    cos = jnp.cos(ang)[None, :, None, :]
    sin = jnp.sin(ang)[None, :, None, :]
    xf = x.astype(F32)
    x1, x2 = xf[..., :half], xf[..., half:]
    return jnp.concatenate([x1 * cos - x2 * sin, x2 * cos + x1 * sin], axis=-1).astype(x.dtype)


def causal_conv(u, buf, w):
    T = u.shape[1]
    full = jnp.concatenate([buf, u], axis=1)
    y = full[:, 0:T] * w[0]
    for j in range(1, CONV_WIDTH):
        y = y + full[:, j:j + T] * w[j]
    return jax.nn.silu(y), full[:, -(CONV_WIDTH - 1):]


def gdn_chunked(q, k, v, g, beta, s0, chunk):
    B, T, H, _ = q.shape
    dv = v.shape[-1]
    n = T // chunk

    def blocks(a):
        return jnp.swapaxes(a.reshape((B, n, chunk, H) + a.shape[3:]), 2, 3)

    q, k, v, g, beta = blocks(q), blocks(k), blocks(v), blocks(g), blocks(beta)
    G = jnp.cumsum(g, axis=-1)
    idx = jnp.arange(chunk)
    incl = idx[:, None] >= idx[None, :]
    strict = idx[:, None] > idx[None, :]
    decay = jnp.exp(jnp.where(incl, G[..., :, None] - G[..., None, :], -jnp.inf))
    kb = k * beta[..., None]
    A = jnp.where(strict, jnp.einsum("bnhid,bnhjd->bnhij", kb, k) * decay, 0.0)
    eye = jnp.eye(chunk, dtype=q.dtype)
    rhs = jnp.concatenate([v * beta[..., None], kb * jnp.exp(G)[..., None]], axis=-1)
    sol = lax.linalg.triangular_solve(eye + A, rhs, left_side=True, lower=True, unit_diagonal=True)
    u_intra, w = sol[..., :dv], sol[..., dv:]
    qk = jnp.einsum("bnhid,bnhjd->bnhij", q, k) * decay
    qg = q * jnp.exp(G)[..., None]
    kd = k * jnp.exp(G[..., -1:] - G)[..., None]
    g_last = jnp.exp(G[..., -1])

    def step(s, xs):
        u_i, w_i, qk_i, qg_i, kd_i, gl_i = xs
        u = u_i - jnp.einsum("bhcd,bhde->bhce", w_i, s)
        o = jnp.einsum("bhcd,bhde->bhce", qg_i, s) + jnp.einsum("bhij,bhje->bhie", qk_i, u)
        s = s * gl_i[..., None, None] + jnp.einsum("bhcd,bhce->bhde", kd_i, u)
        return s, o

    xs = (jnp.moveaxis(u_intra, 1, 0), jnp.moveaxis(w, 1, 0), jnp.moveaxis(qk, 1, 0),
          jnp.moveaxis(qg, 1, 0), jnp.moveaxis(kd, 1, 0), jnp.moveaxis(g_last, 1, 0))
    s_final, o = lax.scan(step, s0, xs)
    o = jnp.swapaxes(jnp.moveaxis(o, 0, 1), 2, 3).reshape(B, T, H, dv)
    return o, s_final


def gdn_mixer(h, conv_buf, s0, w_in, conv_w, a_log, dt_bias, o_norm, w_out):
    B, T, _ = h.shape
    proj = h @ w_in
    i1 = GDN_CONV_CH
    i2 = i1 + GDN_WIDTH
    i3 = i2 + GDN_HEADS
    qkv, z, a, b = proj[..., :i1], proj[..., i1:i2], proj[..., i2:i3], proj[..., i3:]
    qkv_c, new_buf = causal_conv(qkv, conv_buf, conv_w)
    qkv_c = qkv_c.astype(F32)
    q = l2norm(qkv_c[..., :GDN_QK_WIDTH].reshape(B, T, GDN_HEADS, GDN_DK)) * (GDN_DK ** -0.5)
    k = l2norm(qkv_c[..., GDN_QK_WIDTH:2 * GDN_QK_WIDTH].reshape(B, T, GDN_HEADS, GDN_DK))
    v = qkv_c[..., 2 * GDN_QK_WIDTH:].reshape(B, T, GDN_HEADS, GDN_DV)
    g = -jnp.exp(a_log.astype(F32)) * jax.nn.softplus(a.astype(F32) + dt_bias.astype(F32))
    beta = jax.nn.sigmoid(b.astype(F32))
    o, s_new = gdn_chunked(q, k, v, g, beta, s0.astype(F32), min(CHUNK, T))
    o = rmsnorm(o, o_norm) * jax.nn.silu(z.astype(F32).reshape(B, T, GDN_HEADS, GDN_DV))
    y = o.reshape(B, T, GDN_WIDTH).astype(h.dtype) @ w_out
    return y, new_buf, s_new.astype(s0.dtype)


def diff_project(h, w_in, pos):
    B, T, _ = h.shape
    proj = h @ w_in
    i1 = DIFF_QK_WIDTH
    i2 = 2 * DIFF_QK_WIDTH
    i3 = i2 + DIFF_WIDTH
    q = rope(proj[..., :i1].reshape(B, T, 2 * DIFF_HEADS, DIFF_QK_DIM), pos)
    k = rope(proj[..., i1:i2].reshape(B, T, 2 * DIFF_HEADS, DIFF_QK_DIM), pos)
    q = q.reshape(B, T, DIFF_HEADS, 2, DIFF_QK_DIM)
    k = k.reshape(B, T, DIFF_HEADS, 2, DIFF_QK_DIM)
    v = proj[..., i2:i3].reshape(B, T, DIFF_HEADS, DIFF_V_DIM)
    z = proj[..., i3:]
    return q, k, v, z


def diff_lambda(lq1, lk1, lq2, lk2, lam_init):
    return (jnp.exp(jnp.sum(lq1.astype(F32) * lk1.astype(F32)))
            - jnp.exp(jnp.sum(lq2.astype(F32) * lk2.astype(F32))) + lam_init)


def diff_attend(q, k, v, lam, mask):
    s = jnp.einsum("bqhcd,bkhcd->bhcqk", q.astype(F32) * (DIFF_QK_DIM ** -0.5), k.astype(F32))
    if mask is not None:
        s = jnp.where(mask, s, -jnp.inf)
    p = jax.nn.softmax(s, axis=-1)
    a = p[:, :, 0] - lam * p[:, :, 1]
    return jnp.einsum("bhqk,bkhe->bqhe", a, v.astype(F32))


def diff_attend_prompt(q, k, v, lam):
    B, T = q.shape[:2]
    nb = T // Q_BLOCK
    qb = jnp.swapaxes(q.reshape(B, nb, Q_BLOCK, DIFF_HEADS, 2, DIFF_QK_DIM), 0, 1)
    key_chunk = jnp.arange(T) // CHUNK

    def block(args):
        q_blk, bi = args
        q_chunk = (bi * Q_BLOCK + jnp.arange(Q_BLOCK)) // CHUNK
        mask = key_chunk[None, :] <= q_chunk[:, None]
        return diff_attend(q_blk, k, v, lam, mask)

    o = lax.map(block, (qb, jnp.arange(nb)))
    return jnp.swapaxes(o, 0, 1).reshape(B, T, DIFF_HEADS, DIFF_V_DIM)


def diff_output(o, z, subln, lam_init, w_out, dtype):
    B, T = o.shape[:2]
    o = rmsnorm(o, subln) * (1.0 - lam_init)
    o = o.reshape(B, T, DIFF_WIDTH) * jax.nn.silu(z.astype(F32))
    return o.astype(dtype) @ w_out


def trunk(x, c, pos, conv_bufs, gdn_states, past_k, past_v, p):
    B, T, _ = x.shape
    c_act = jax.nn.silu(c.astype(F32))
    states, convs, ks, vs = [], [], [], []
    for i in range(DEPTH):
        j = i // N_MIXERS
        ada = c_act @ p["w_ada"][i].astype(F32) + p["b_ada"][i].astype(F32)
        shift, scale, gate = jnp.split(ada[:, None, :], 3, axis=-1)
        h = (rmsnorm(x, p["norm_pre"][i]).astype(F32) * (1.0 + scale) + shift).astype(x.dtype)
        if i % N_MIXERS == 0:
            out, buf, st = gdn_mixer(h, conv_bufs[j], gdn_states[j], p["w_in_gdn"][j], p["conv_gdn"][j],
                                     p["a_log_gdn"][j], p["dt_bias_gdn"][j], p["onorm_gdn"][j],
                                     p["w_out_gdn"][j])
            convs.append(buf)
            states.append(st)
        else:
            lam_init = 0.8 - 0.6 * math.exp(-0.3 * i)
            lam = diff_lambda(p["lam_q1"][j], p["lam_k1"][j], p["lam_q2"][j], p["lam_k2"][j], lam_init)
            q, k, v, z = diff_project(h, p["w_in_diff"][j], pos)
            if past_k is None:
                o = diff_attend_prompt(q, k, v, lam)
            else:
                k_all = jnp.concatenate([past_k[j].reshape(B, -1, DIFF_HEADS, 2, DIFF_QK_DIM), k], axis=1)
                v_all = jnp.concatenate([past_v[j], v], axis=1)
                o = diff_attend(q, k_all, v_all, lam, None)
            out = diff_output(o, z, p["subln_diff"][j], lam_init, p["w_out_diff"][j], x.dtype)
            ks.append(k.reshape(B, T, DIFF_HEADS, 2 * DIFF_QK_DIM))
            vs.append(v)
        x = x + (gate * rmsnorm(out, p["norm_post"][i]).astype(F32)).astype(x.dtype)
    return x, jnp.stack(states), jnp.stack(convs), jnp.stack(ks), jnp.stack(vs)


def setup_inputs(seed: int = 0) -> dict:
    key = jax.random.key(seed)
    ks = iter(jax.random.split(key, 32))

    def nrm(shape, s=1.0):
        return jax.random.normal(next(ks), shape, F32) * s

    def unif(shape, lo, hi):
        return jax.random.uniform(next(ks), shape, F32, lo, hi)

    dt = jnp.exp(unif((N_GDN, GDN_HEADS), math.log(1e-3), math.log(1e-1)))
    return {
        "x_prompt": nrm((BATCH, SEQ, D_MODEL)),
        "x_sample": nrm((DEC_BATCH, DEC_SEQ, D_MODEL)),
        "c_prompt": nrm((BATCH, D_MODEL)),
        "c_sample": nrm((DEC_BATCH, D_MODEL)),
        "state_gdn": nrm((N_GDN, DEC_BATCH, GDN_HEADS, GDN_DK, GDN_DV), 0.1),
        "cache_conv": nrm((N_GDN, DEC_BATCH, CONV_WIDTH - 1, GDN_CONV_CH)),
        "cache_k": nrm((N_DIFF, DEC_BATCH, PAST_LEN, DIFF_HEADS, 2 * DIFF_QK_DIM)),
        "cache_v": nrm((N_DIFF, DEC_BATCH, PAST_LEN, DIFF_HEADS, DIFF_V_DIM)),
        "norm_pre": 1.0 + nrm((DEPTH, D_MODEL), 0.02),
        "norm_post": 1.0 + nrm((DEPTH, D_MODEL), 0.02),
        "w_ada": nrm((DEPTH, D_MODEL, 3 * D_MODEL), 0.5 * D_MODEL ** -0.5),
        "b_ada": nrm((DEPTH, 3 * D_MODEL), 0.02),
        "w_in_gdn": nrm((N_GDN, D_MODEL, GDN_IN), D_MODEL ** -0.5),
        "conv_gdn": nrm((N_GDN, CONV_WIDTH, GDN_CONV_CH), CONV_WIDTH ** -0.5),
        "a_log_gdn": jnp.log(unif((N_GDN, GDN_HEADS), 1.0, 16.0)),
        "dt_bias_gdn": dt + jnp.log(-jnp.expm1(-dt)),
        "onorm_gdn": 1.0 + nrm((N_GDN, GDN_DV), 0.02),
        "w_out_gdn": nrm((N_GDN, GDN_WIDTH, D_MODEL), GDN_WIDTH ** -0.5),
        "w_in_diff": nrm((N_DIFF, D_MODEL, DIFF_IN), D_MODEL ** -0.5),
        "lam_q1": nrm((N_DIFF, DIFF_QK_DIM), 0.1),
        "lam_k1": nrm((N_DIFF, DIFF_QK_DIM), 0.1),
        "lam_q2": nrm((N_DIFF, DIFF_QK_DIM), 0.1),
        "lam_k2": nrm((N_DIFF, DIFF_QK_DIM), 0.1),
        "subln_diff": 1.0 + nrm((N_DIFF, DIFF_V_DIM), 0.02),
        "w_out_diff": nrm((N_DIFF, DIFF_WIDTH, D_MODEL), DIFF_WIDTH ** -0.5),
    }


def reference(x_prompt, x_sample, c_prompt, c_sample, state_gdn, cache_conv, cache_k, cache_v,
              norm_pre, norm_post, w_ada, b_ada, w_in_gdn, conv_gdn, a_log_gdn, dt_bias_gdn,
              onorm_gdn, w_out_gdn, w_in_diff, lam_q1, lam_k1, lam_q2, lam_k2, subln_diff, w_out_diff):
    p = {"norm_pre": norm_pre, "norm_post": norm_post, "w_ada": w_ada, "b_ada": b_ada,
         "w_in_gdn": w_in_gdn, "conv_gdn": conv_gdn, "a_log_gdn": a_log_gdn,
         "dt_bias_gdn": dt_bias_gdn, "onorm_gdn": onorm_gdn, "w_out_gdn": w_out_gdn,
         "w_in_diff": w_in_diff, "lam_q1": lam_q1, "lam_k1": lam_k1, "lam_q2": lam_q2,
         "lam_k2": lam_k2, "subln_diff": subln_diff, "w_out_diff": w_out_diff}
    Bp, Tp, _ = x_prompt.shape
    Ts = x_sample.shape[1]
    zero_conv = jnp.zeros((N_GDN, Bp, CONV_WIDTH - 1, GDN_CONV_CH), x_prompt.dtype)
    zero_state = jnp.zeros((N_GDN, Bp, GDN_HEADS, GDN_DK, GDN_DV), x_prompt.dtype)
    y_prompt, st_p, conv_p, k_p, v_p = trunk(x_prompt, c_prompt, jnp.arange(Tp), zero_conv, zero_state,
                                             None, None, p)
    y_sample, st_s, conv_s, k_s, v_s = trunk(x_sample, c_sample, PAST_LEN + jnp.arange(Ts), cache_conv,
                                             state_gdn, cache_k, cache_v, p)
    return (y_prompt, y_sample, st_p, conv_p, k_p, v_p, st_s, conv_s, k_s, v_s)
```

```python
import contextlib
import math
import numpy as np
import concourse.bass as bass
import concourse.mybir as mybir
from concourse.bass_utils import run_bass_kernel_spmd

F32 = mybir.dt.float32
BF16 = mybir.dt.bfloat16
AF = mybir.ActivationFunctionType
ALU = mybir.AluOpType
AX = mybir.AxisListType

D = 1024
KC = 8
H = 8
DEPTH = 4
GIN = 4112
DIN = 4096
EPS = 1e-6
NEG = -1.0e9
import os
DBG_NOSAMPLE = bool(os.environ.get('DBG_NOSAMPLE'))
DBG = {}
LAGS = tuple(int(c) for c in os.environ.get('LAGS', '0123'))
EMBED_WAIT = not os.environ.get('NO_EMBED')
DBG_D = int(os.environ.get('DBG_D', '99'))
DBG_CUT = int(os.environ.get('DBG_CUT', '99'))
DBG_STAGE = int(os.environ.get('DBG_STAGE', '99'))


class Ctr:
    LIM = 30000

    def __init__(self, k, name):
        self.k = k
        self.name = name
        self.sems = []
        self.n = 0

    def sem(self, epoch):
        while len(self.sems) <= epoch:
            s = self.k.perm.enter_context(self.k.nc.semaphore("%s_%d" % (self.name, len(self.sems))))
            self.sems.append(s)
        return self.sems[epoch]

    def bump(self, inc):
        if (self.n % self.LIM) + inc > self.LIM:
            self.n = (self.n // self.LIM + 1) * self.LIM
        e = self.n // self.LIM
        self.n += inc
        return self.sem(e), self.n

    def target(self, val):
        e = (val - 1) // self.LIM
        return self.sem(e), val - e * self.LIM


class V:
    __slots__ = ("b", "ap")

    def __init__(self, b, ap):
        self.b = b
        self.ap = ap


class Buf:
    def __init__(self, t, name, dma=False, k=None):
        self.t = t
        self.name = name
        self.w = None
        self.r = {}
        self.psum = False
        self.ctr = Ctr(k, "d_" + name) if dma else None

    def __getitem__(self, key):
        return V(self, self.t[key])

    def v(self, ap):
        return V(self, ap)


class Eng:
    def __init__(self, k, name, e):
        self.name = name
        self.e = e
        self.ctr = Ctr(k, "e_" + name)
        self.waited = {}


class K:
    def __init__(self, nc, stack):
        self.nc = nc
        self.stack = stack
        self.perm = stack
        self.held = set()
        self.nname = 0
        self.pe = Eng(self, "pe", nc.tensor)
        self.act = Eng(self, "act", nc.scalar)
        self.dve = Eng(self, "dve", nc.vector)
        self.pool = Eng(self, "pool", nc.gpsimd)
        self.sp = Eng(self, "sp", nc.sync)
        self.engs = [self.pe, self.act, self.dve, self.pool, self.sp]
        self.dma_ctrs = []
        self.nps = 0
        self.nins = 0

    def sb(self, name, shape, dt, dma=False):
        self.nname += 1
        name = "%s_%d" % (name, self.nname)
        t = self.stack.enter_context(self.nc.sbuf_tensor(name, list(shape), dt))
        b = Buf(t, name, dma, self)
        if dma:
            self.dma_ctrs.append(b.ctr)
        return b

    def dram(self, ap, name):
        return Buf(ap, name, False, self)

    def _wait(self, eng, ctr, val):
        if val <= 0:
            return
        if eng.waited.get(ctr, 0) >= val:
            return
        sem, v = ctr.target(val)
        eng.e.wait_ge(sem, v)
        eng.waited[ctr] = val

    def op(self, eng, fn, outs, ins, dma_buf=None):
        deps = []
        for v in ins:
            if v.b.w is not None:
                deps.append(v.b.w)
            if v.b.psum:
                for c, val in v.b.r.items():
                    if c is not eng.ctr:
                        deps.append((c, val))
        own = None
        for v in outs:
            if v.b.w is not None and v.b.w[0] is not own:
                deps.append(v.b.w)
            for c, val in v.b.r.items():
                if c is not own:
                    deps.append((c, val))
        need = {}
        for c, val in deps:
            if eng is self.pe and c is self.pe.ctr:
                continue
            if val > 0 and eng.waited.get(c, 0) < val and need.get(c, 0) < val:
                need[c] = val
        if dma_buf is not None and dma_buf.ctr.n > 0 and eng.waited.get(dma_buf.ctr, 0) < dma_buf.ctr.n:
            need[dma_buf.ctr] = dma_buf.ctr.n
        need = list(need.items())
        embed = None
        if EMBED_WAIT and dma_buf is None and need:
            embed = need.pop()
        for c, val in need:
            self._wait(eng, c, val)
        ins_obj = fn()
        if embed is not None:
            sem, v = embed[0].target(embed[1])
            ins_obj._wait_ge(sem, v)
            eng.waited[embed[0]] = embed[1]
        if dma_buf is not None:
            ctr = dma_buf.ctr
            sem, val = ctr.bump(16)
            ins_obj.then_inc(sem, 16)
        else:
            ctr = eng.ctr
            sem, val = ctr.bump(1)
            ins_obj.then_inc(sem, 1)
        for v in ins:
            v.b.r[ctr] = val
        for v in outs:
            v.b.w = (ctr, val)
            v.b.r = {}
        self.nins += 1

    def barrier(self):
        ctrs = [e.ctr for e in self.engs[:4]] + self.dma_ctrs
        for e in self.engs:
            for c in ctrs:
                if c is e.ctr and e is self.pe:
                    continue
                self._wait(e, c, c.n)

    def mm(self, out, lhsT, rhs, start=True, stop=True):
        self.op(self.pe, lambda: self.nc.tensor.matmul(out.ap, lhsT=lhsT.ap, rhs=rhs.ap, start=start, stop=stop),
                [out], [lhsT, rhs])

    def tr(self, out, in_, ident):
        self.op(self.pe, lambda: self.nc.tensor.transpose(out.ap, in_.ap, ident.ap), [out], [in_, ident])

    def actf(self, out, in_, func, bias=0.0, scale=1.0, accum=None):
        ins = [in_]
        outs = [out]
        kw = {}
        if isinstance(bias, V):
            ins.append(bias)
            kw["bias"] = bias.ap
        else:
            kw["bias"] = float(bias)
        if isinstance(scale, V):
            ins.append(scale)
            kw["scale"] = scale.ap
        else:
            kw["scale"] = float(scale)
        if accum is not None:
            outs.append(accum)
            kw["accum_out"] = accum.ap
        self.op(self.act, lambda: self.nc.scalar.activation(out=out.ap, in_=in_.ap, func=func, **kw), outs, ins)

    def _e(self, eng):
        return {"dve": self.dve, "pool": self.pool, "act": self.act}[eng]

    def tt(self, eng, out, in0, in1, op):
        e = self._e(eng)
        self.op(e, lambda: e.e.tensor_tensor(out=out.ap, in0=in0.ap, in1=in1.ap, op=op), [out], [in0, in1])

    def ts(self, eng, out, in0, s1, op0, s2=None, op1=None):
        e = self._e(eng)
        ins = [in0]
        a1 = s1
        a2 = s2
        if isinstance(s1, V):
            ins.append(s1)
            a1 = s1.ap
        if isinstance(s2, V):
            ins.append(s2)
            a2 = s2.ap
        if op1 is None:
            self.op(e, lambda: e.e.tensor_scalar(out=out.ap, in0=in0.ap, scalar1=a1, scalar2=None, op0=op0),
                    [out], ins)
        else:
            self.op(e, lambda: e.e.tensor_scalar(out=out.ap, in0=in0.ap, scalar1=a1, scalar2=a2, op0=op0, op1=op1),
                    [out], ins)

    def stt(self, eng, out, in0, s, in1, op0, op1):
        e = self._e(eng)
        ins = [in0, in1]
        a = s
        if isinstance(s, V):
            ins.append(s)
            a = s.ap
        self.op(e, lambda: e.e.scalar_tensor_tensor(out=out.ap, in0=in0.ap, scalar=a, in1=in1.ap, op0=op0, op1=op1),
                [out], ins)

    def cp(self, eng, out, in_):
        if eng == "act":
            self.op(self.act, lambda: self.nc.scalar.copy(out=out.ap, in_=in_.ap), [out], [in_])
        elif eng == "dve" and in_.b.name.startswith("ps"):
            self.ts("dve", out, in_, 1.0, ALU.mult)
        else:
            e = self._e(eng)
            self.op(e, lambda: e.e.tensor_copy(out=out.ap, in_=in_.ap), [out], [in_])

    def memset(self, eng, out, val):
        e = self._e(eng)
        self.op(e, lambda: e.e.memset(out.ap, val), [out], [])

    def recip(self, out, in_):
        self.op(self.dve, lambda: self.nc.vector.reciprocal(out=out.ap, in_=in_.ap), [out], [in_])

    def rsum(self, out, in_):
        self.op(self.dve, lambda: self.nc.vector.reduce_sum(out=out.ap, in_=in_.ap, axis=AX.X), [out], [in_])

    def dma(self, out, in_, q=None, slow=False):
        q = q or self.sp
        owner = out.b if out.b.ctr is not None else in_.b
        assert owner.ctr is not None, (out.b.name, in_.b.name)
        kw = {"allow_slow_non_contiguous": True} if slow else {}
        self.op(q, lambda: q.e.dma_start(out=out.ap, in_=in_.ap, **kw), [out], [in_], dma_buf=owner)


def build_program(T, PAST, NSB, TS=16, depth=DEPTH):
    nc = bass.Bass("TRN2", target_bir_lowering=False)
    NT = T // 128
    NPT = PAST // 128
    n_gdn = (depth + 1) // 2
    n_diff = depth // 2

    def din(name, shape, dt=F32):
        return nc.dram_tensor(name, list(shape), dt, kind="ExternalInput").ap()

    def dout(name, shape, dt=F32):
        return nc.dram_tensor(name, list(shape), dt, kind="ExternalOutput").ap()

    def dscr(name, shape, dt=F32):
        return nc.dram_tensor(name, list(shape), dt).ap()

    xp = din("xp", [T, D])
    xs = din("xs", [NSB, TS, D])
    c3 = din("c3", [128, KC, 1 + NSB])
    st_in = din("st_in", [2, NSB, H, 128, 128])
    cc_in = din("cc_in", [2, NSB, 128, 24, 3])
    ck_in = din("ck_in", [2, NSB, H, 128, PAST])
    cv_in = din("cv_in", [2, NSB, PAST, H, 128])
    norm_pre = din("norm_pre", [DEPTH, D])
    norm_post = din("norm_post", [DEPTH, D])
    w_ada = din("w_ada", [DEPTH, 128, KC, 3 * D])
    b_ada = din("b_ada", [DEPTH, 3 * D])
    w_in_gdn = din("w_in_gdn", [2, 128, KC, GIN])
    conv_w = din("conv_w", [2, 128, 24, 4])
    a_log = din("a_log", [2, H])
    dt_bias = din("dt_bias", [2, H])
    onorm = din("onorm", [2, 128])
    w_out_gdn = din("w_out_gdn", [2, 128, KC, D])
    w_in_diff = din("w_in_diff", [2, 128, KC, DIN])
    lamv = din("lamv", [2, 4, 64])
    subln = din("subln", [2, 128])
    w_out_diff = din("w_out_diff", [2, 128, KC, D])
    cst = din("cst", [5, 128, 128])
    lvmask = din("lvmask", [14, 128, 128])
    rope_p = din("rope_p", [2, T, 32])
    rope_s = din("rope_s", [2, TS, 32])
    y_p = dout("y_p", [T, D])
    y_s = dout("y_s", [NSB, TS, D])
    st_p = dout("st_p", [n_gdn, H, 128, 128])
    conv_p = dout("conv_p", [n_gdn, 3, 3072])
    k_p = dout("k_p", [max(n_diff, 1), T, D])
    v_p = dout("v_p", [max(n_diff, 1), T, D])
    st_s = dout("st_s", [n_gdn, NSB, H, 128, 128])
    conv_s = dout("conv_s", [n_gdn, NSB, 3, 3072])
    k_s = dout("k_s", [max(n_diff, 1), NSB, TS, D])
    v_s = dout("v_s", [max(n_diff, 1), NSB, TS, D])
    xa = dscr("xa", [T, D])
    xsa = dscr("xsa", [NSB, TS, D])
    ada_d = dscr("ada_d", [DEPTH, 1 + NSB, 3 * D])
    qT_d = dscr("qT_d", [H, 128, T], BF16)
    kT_d = dscr("kT_d", [H, 128, T], BF16)
    vb_d = dscr("vb_d", [H, T, 128], BF16)
    z_d = dscr("z_d", [T, D])
    o_d = dscr("o_d", [T, D])

    with contextlib.ExitStack() as stack:
        k = K(nc, stack)
        Dx = [k.dram(None, "dx%d" % t) for t in range(NT)]
        Dxs = [k.dram(None, "dxs%d" % s) for s in range(NSB)]
        Dada = k.dram(None, "dada")
        Dq = [k.dram(None, "dq%d" % h) for h in range(H)]
        Dz = [k.dram(None, "dz%d" % t) for t in range(NT)]
        Do = [k.dram(None, "do%d" % t) for t in range(NT)]
        Dout = k.dram(None, "dout")
        Din = k.dram(None, "din")

        banks = []
        for i in range(8):
            t = stack.enter_context(nc.psum_tensor("ps%d" % i, [128, 512], F32))
            banks.append(Buf(t, "ps%d" % i, False, k))
            banks[-1].psum = True

        def ps(hold=False):
            while (k.nps % 8) in k.held:
                k.nps += 1
            i = k.nps % 8
            k.nps += 1
            if hold:
                k.held.add(i)
            return banks[i]

        def unhold(bs):
            for b in bs:
                k.held.discard(banks.index(b))

        def psf(b, p, n):
            return b.v(b.t[0:p, 0:n])

        def psb(b):
            return b.t[:].bitcast(BF16)

        cst_f = k.sb("cst_f", [128, 5, 128], F32, dma=True)
        k.dma(cst_f[:, :, :], Din.v(cst.rearrange("c p f -> p c f")))
        ident_f = cst_f.v(cst_f.t[:, 0, :])
        U_f = cst_f.v(cst_f.t[:, 1, :])
        ones_f = cst_f.v(cst_f.t[:, 4, :])
        ident_b = k.sb("ident_b", [128, 128], BF16)
        ones_b = k.sb("ones_b", [128, 128], BF16)
        k.cp("dve", ident_b[:, :], ident_f)
        k.cp("dve", ones_b[:, :], ones_f)

        def mneg_T(C):
            return cst_f.v(cst_f.t[0:C, 2, 0:C])

        def mpos_S(C):
            return cst_f.v(cst_f.t[0:C, 3, 0:C])

        W_in = k.sb("W_in", [128, KC, GIN], BF16)
        W_out = k.sb("W_out", [128, KC, D], BF16)
        stg = None
        wvec = k.sb("wvec", [128, D], F32, dma=True)
        shiftv = k.sb("shiftv", [128, D], F32, dma=True)
        gvec = k.sb("gvec", [128, D], F32, dma=True)
        xt0 = k.sb("xt0", [128, D], F32, dma=True)
        xt = [xt0, xt0]
        t1 = k.sb("t1", [128, D], F32, dma=True)
        hb = k.sb("hb", [128, D], BF16)
        hT = k.sb("hT", [128, KC, 128], BF16)
        og = k.sb("og", [128, D], BF16)
        ogT = k.sb("ogT", [128, KC, 128], BF16)
        col = k.sb("col", [128, 16], F32)
        stg_i = [0]

        def load_weight(dst, src_ap, ncols, stg_=None):
            CH = 1028
            for kc in range(KC):
                for c0 in range(0, ncols, CH):
                    cw = min(CH, ncols - c0)
                    s = (stg_ or stg)[stg_i[0] % 2]
                    stg_i[0] += 1
                    k.dma(s[:, 0:cw], Din.v(src_ap[:, kc, c0:c0 + cw]))
                    eng = "dve" if (stg_i[0] % 2 == 0) else "pool"
                    k.cp(eng, dst[:, kc, c0:c0 + cw], s[:, 0:cw])

        NS = 1 + NSB
        with contextlib.ExitStack() as st2:
            k2 = k
            old = k.stack
            k.stack = st2
            c3_sb = k.sb("c3_sb", [128, KC, NS], F32, dma=True)
            cact = k.sb("cact", [128, KC, NS], F32)
            wst = [k.sb("wst%d" % i, [128, 1536], F32, dma=True) for i in range(2)]
            bada = k.sb("bada", [NS, 3 * D], F32, dma=True)
            arow = k.sb("arow", [NS, 3 * D], F32, dma=True)
            k.dma(c3_sb[:, :, :], Din.v(c3))
            k.actf(cact[:, :, :], c3_sb[:, :, :], AF.Silu)
            wi = 0
            for l in range(depth):
                k.dma(bada[:, :], Din.v(b_ada[l:l + 1, :].to_broadcast([NS, 3 * D])))
                for half in range(2):
                    pbs = [ps() for _ in range(3)]
                    for kc in range(KC):
                        s = wst[wi % 2]
                        wi += 1
                        k.dma(s[:, :], Din.v(w_ada[l, :, kc, half * 1536:(half + 1) * 1536]))
                        for j in range(3):
                            k.mm(psf(pbs[j], NS, 512), cact[:, kc, :], s[:, j * 512:(j + 1) * 512],
                                 start=(kc == 0), stop=(kc == KC - 1))
                    for j in range(3):
                        c0 = half * 1536 + j * 512
                        k.tt("dve", arow[:, c0:c0 + 512], psf(pbs[j], NS, 512), bada[:, c0:c0 + 512], ALU.add)
                k.dma(Dada.v(ada_d[l, :, :]), arow[:, :])
            k.barrier()
            k.stack = old

        def load_mod(l, s, C):
            k.dma(shiftv[0:C, :], Dada.v(ada_d[l, s:s + 1, 0:D].to_broadcast([C, D])))
            k.dma(wvec[0:C, :], Dada.v(ada_d[l, s:s + 1, D:2 * D].to_broadcast([C, D])))
            k.dma(gvec[0:C, :], Dada.v(ada_d[l, s:s + 1, 2 * D:3 * D].to_broadcast([C, D])))
            k.dma(t1[0:C, :], Din.v(norm_pre[l:l + 1, :].to_broadcast([C, D])))
            k.stt("dve", wvec[0:C, :], wvec[0:C, :], 1.0, t1[0:C, :], ALU.add, ALU.mult)
            k.dma(t1[0:C, :], Din.v(norm_post[l:l + 1, :].to_broadcast([C, D])))
            k.tt("dve", gvec[0:C, :], gvec[0:C, :], t1[0:C, :], ALU.mult)

        def rstd_from_ss(ssv, C, n, dst):
            k.actf(dst, ssv, AF.Ln, bias=EPS, scale=1.0 / n)
            k.actf(dst, dst, AF.Exp, scale=-0.5)

        def norm_mod_T(x, C):
            k.actf(t1[0:C, :], x[0:C, :], AF.Square, accum=col[0:C, 0:1])
            rstd_from_ss(col[0:C, 0:1], C, D, col[0:C, 1:2])
            k.stt("dve", t1[0:C, :], x[0:C, :], col[0:C, 1:2], wvec[0:C, :], ALU.mult, ALU.mult)
            k.tt("pool", hb[0:C, :], t1[0:C, :], shiftv[0:C, :], ALU.add)
            to_T(hb, hT, C)

        def to_T(src, dst, C):
            pb = ps()
            pv = psb(pb)
            for kc in range(KC):
                k.tr(pb.v(pv[:, kc * 128:kc * 128 + C]), src[0:C, kc * 128:(kc + 1) * 128], ident_b[0:C, 0:C])
            k.cp("act", dst[:, :, 0:C], pb.v(pv.rearrange("p (k c) -> p k c", k=KC)[:, :, 0:C]))

        def out_proj_post(x, C, dst_dram_v):
            to_T(og, ogT, C)
            pbs = [ps(), ps()]
            for j in range(2):
                for kc in range(KC):
                    k.mm(psf(pbs[j], C, 512), ogT[:, kc, 0:C], W_out[:, kc, j * 512:(j + 1) * 512],
                         start=(kc == 0), stop=(kc == KC - 1))
            for j in range(2):
                k.actf(t1[0:C, j * 512:(j + 1) * 512], psf(pbs[j], C, 512), AF.Square, accum=col[0:C, 2 + j:3 + j])
            k.tt("dve", col[0:C, 4:5], col[0:C, 2:3], col[0:C, 3:4], ALU.add)
            rstd_from_ss(col[0:C, 4:5], C, D, col[0:C, 5:6])
            for j in range(2):
                k.stt("dve", t1[0:C, j * 512:(j + 1) * 512], psf(pbs[j], C, 512), col[0:C, 5:6],
                      gvec[0:C, j * 512:(j + 1) * 512], ALU.mult, ALU.mult)
            k.tt("pool", t1[0:C, :], t1[0:C, :], x[0:C, :], ALU.add)
            k.dma(dst_dram_v, t1[0:C, :])

        def gdn_layer(l, j, src_p, dst_p, src_s, dst_s):
            with contextlib.ExitStack() as st2:
                old = k.stack
                k.stack = st2
                HG = 4
                Dg = k.sb("Dg", [128, 24, 4, 128], BF16)
                msk = k.sb("msk", [128, 14, 128], BF16, dma=False)
                with contextlib.ExitStack() as st3:
                    k.stack = st3
                    stg_l = [k.sb("stgl%d" % i, [128, 1028], F32, dma=True) for i in range(2)]
                    mskf = k.sb("mskf", [128, 14, 128], F32, dma=True)
                    k.dma(mskf[:, :, :], Din.v(lvmask.rearrange("c p f -> p c f")))
                    k.cp("pool", msk[:, :, :], mskf[:, :, :])
                    load_weight(W_in, w_in_gdn[j], GIN, stg_l)
                    load_weight(W_out, w_out_gdn[j], D, stg_l)
                    k.barrier()
                    k.stack = st2
                cw = k.sb("cw", [128, 24, 4], F32, dma=True)
                uext = k.sb("uext", [128, 24, 131], BF16)
                uext_s = k.sb("uext_s", [128, 24, 3 + TS], BF16)
                cst32 = k.sb("cst32", [128, 24, 3], F32, dma=True)
                zs = k.sb("zs", [128, D], F32)
                gsm = k.sb("gsm", [128, 12, H], F32, dma=True)
                Gbc = k.sb("Gbc", [128, H, 128], F32)
                eGbc = k.sb("eGbc", [128, H, 128], BF16)
                onb = k.sb("onb", [128, 128], F32, dma=True)
                Sf = k.sb("Sf", [128, H, 128], F32, dma=True)
                Sb = k.sb("Sb", [128, H, 128], BF16)

                def mk(name, shape, dt):
                    return [k.sb("%s_%d" % (name, i), shape, dt) for i in range(HG)]
                qkv = mk("qkv", [128, 3, 128], F32)
                sq = mk("sq", [128, 128], BF16)
                rn = mk("rn", [128, 128], F32)
                qT = mk("qT", [128, 128], BF16)
                kT = mk("kT", [128, 128], BF16)
                vbf = mk("vbf", [128, 128], BF16)
                kd = mk("kd", [128, 128], BF16)
                kbg = mk("kbg", [128, 128], BF16)
                vb = mk("vb", [128, 128], BF16)
                arg = mk("arg", [128, 128], F32)
                decT = mk("decT", [128, 128], BF16)
                dec = mk("dec", [128, 128], BF16)
                Am = [mk("Am%d" % i, [128, 128], BF16) for i in range(2)]
                Bm = [mk("Bm%d" % i, [128, 128], BF16) for i in range(2)]
                Pm = [mk("Pm%d" % i, [128, 128], BF16) for i in range(2)]
                Dm = [mk("Dm%d" % i, [128, 128], BF16) for i in range(2)]
                Xm = [mk("Xm%d" % i, [128, 128], BF16) for i in range(2)]
                ui = mk("ui", [128, 128], F32)
                wT = mk("wT", [128, 128], BF16)
                qgT = mk("qgT", [128, 128], BF16)
                qkT = mk("qkT", [128, 128], BF16)
                ub = mk("ub", [128, 128], BF16)
                onf = mk("onf", [128, 128], F32)

                k.dma(cw[:, :, :], Din.v(conv_w[j]))
                for c in range(24):
                    for tap in range(4):
                        k.ts("dve" if (c + tap) % 2 else "pool", Dg[:, c, tap, :], ident_f, cw[:, c, tap:tap + 1], ALU.mult)
                k.dma(gsm[:, 0, :], Din.v(dt_bias[j:j + 1, :].to_broadcast([128, H])))
                k.dma(gsm[:, 1, :], Din.v(a_log[j:j + 1, :].to_broadcast([128, H])))
                k.actf(gsm[:, 1, :], gsm[:, 1, :], AF.Exp)
                k.ts("dve", gsm[:, 1, :], gsm[:, 1, :], -1.0, ALU.mult)
                k.dma(onb[:, :], Din.v(onorm[j:j + 1, :].to_broadcast([128, 128])))

                def gUv(hh, C):
                    return t1.v(t1.t[0:C, :].rearrange("p (h c) -> p h c", h=H)[:, hh, 0:C])

                def gdn_tile(x, C, ue):
                    norm_mod_T(x, C)
                    if DBG_CUT <= 1:
                        return
                    for g in range(6):
                        pb = ps()
                        for c4 in range(4):
                            ch = 4 * g + c4
                            for kc in range(KC):
                                k.mm(pb.v(pb.t[:, c4 * 128:c4 * 128 + C]), W_in[:, kc, ch * 128:(ch + 1) * 128],
                                     hT[:, kc, 0:C], start=(kc == 0), stop=(kc == KC - 1))
                        k.cp("act", ue[:, 4 * g:4 * g + 4, 3:3 + C],
                             pb.v(pb.t[:].rearrange("p (a c) -> p a c", a=4)[:, :, 0:C]))
                    if DBG_CUT <= 2:
                        return
                    for jb in range(2):
                        pb = ps()
                        for kc in range(KC):
                            k.mm(psf(pb, C, 512), hT[:, kc, 0:C], W_in[:, kc, 3072 + jb * 512:3072 + (jb + 1) * 512],
                                 start=(kc == 0), stop=(kc == KC - 1))
                        k.actf(zs[0:C, jb * 512:(jb + 1) * 512], psf(pb, C, 512), AF.Silu)
                    pab = ps()
                    for kc in range(KC):
                        k.mm(psf(pab, C, 16), hT[:, kc, 0:C], W_in[:, kc, 4096:4112], start=(kc == 0), stop=(kc == KC - 1))
                    G = lambda i: gsm[0:C, i, :]
                    k.tt("dve", G(2), pab.v(pab.t[0:C, 0:H]), G(0), ALU.add)
                    k.actf(G(2), G(2), AF.Exp)
                    k.actf(G(2), G(2), AF.Ln, bias=1.0)
                    k.tt("dve", G(3), G(2), G(1), ALU.mult)
                    k.actf(G(4), pab.v(pab.t[0:C, H:2 * H]), AF.Exp, scale=-1.0)
                    k.ts("dve", G(4), G(4), 1.0, ALU.add)
                    k.recip(G(4), G(4))
                    pg = ps()
                    k.mm(psf(pg, C, H), cst_f.v(cst_f.t[0:C, 1, 0:C]), G(3))
                    k.cp("dve", G(5), psf(pg, C, H))
                    k.mm(pg.v(pg.t[:, 16:16 + H]), cst_f.v(cst_f.t[0:C, 4, :]), G(3))
                    k.cp("dve", gsm[:, 6, :], pg.v(pg.t[:, 16:16 + H]))
                    k.actf(gsm[:, 7, :], gsm[:, 6, :], AF.Exp)
                    k.actf(G(8), G(5), AF.Exp)
                    k.tt("dve", G(8), G(8), G(4), ALU.mult)
                    k.tt("dve", G(9), gsm[0:C, 6, :], G(5), ALU.subtract)
                    k.actf(G(9), G(9), AF.Exp)
                    if DBG_CUT <= 3:
                        return
                    for hh in range(H):
                        k.ts("dve" if hh % 2 else "pool", gUv(hh, C), cst_f.v(cst_f.t[0:C, 1, 0:C]),
                             gsm[0:C, 3, hh:hh + 1], ALU.mult)
                    hpm = max(1, 512 // C)
                    for h0 in range(0, H, hpm):
                        hn = min(hpm, H - h0)
                        pb = ps()
                        for hh in range(hn):
                            k.mm(pb.v(pb.t[:, hh * C:(hh + 1) * C]), cst_f.v(cst_f.t[0:C, 4, :]), gUv(h0 + hh, C))
                        k.cp("act", Gbc[:, h0:h0 + hn, 0:C], pb.v(pb.t[:, 0:hn * C].rearrange("p (h c) -> p h c", h=hn)))
                        k.actf(eGbc[:, h0:h0 + hn, 0:C], pb.v(pb.t[:, 0:hn * C].rearrange("p (h c) -> p h c", h=hn)), AF.Exp)
                    if DBG_CUT <= 4:
                        return
                    nlev = 0
                    while (1 << nlev) < C:
                        nlev += 1

                    def s_conv(h, s, st):
                        pc = ps()
                        for i3, ch in enumerate((h, 8 + h, 16 + h)):
                            for tap in range(4):
                                k.mm(pc.v(pc.t[:, i3 * 128:i3 * 128 + C]), Dg[:, ch, tap, :], ue[:, ch, tap:tap + C],
                                     start=(tap == 0), stop=(tap == 3))
                        yield
                        k.actf(qkv[s][:, :, 0:C], pc.v(pc.t[:, 0:384].rearrange("p (a c) -> p a c", a=3)[:, :, 0:C]), AF.Silu)

                    def s_norm(i2):
                        def f(h, s, st):
                            dstT, sc = ((qT, 128 ** -0.5), (kT, 1.0))[i2]
                            k.actf(sq[s][:, 0:C], qkv[s][:, i2, 0:C], AF.Square)
                            if i2 == 1:
                                k.cp("pool", vbf[s][:, 0:C], qkv[s][:, 2, 0:C])
                            yield
                            pn = ps()
                            k.mm(pn.v(pn.t[:, 0:C]), ones_b[:, :], sq[s][:, 0:C])
                            yield
                            k.actf(rn[s][:, 0:C], pn.v(pn.t[:, 0:C]), AF.Ln, bias=1e-6)
                            yield
                            k.actf(rn[s][:, 0:C], rn[s][:, 0:C], AF.Exp, scale=-0.5)
                            yield
                            k.stt("dve", dstT[s][:, 0:C], qkv[s][:, i2, 0:C], sc, rn[s][:, 0:C], ALU.mult, ALU.mult)
                        return f

                    def s_tok(h, s, st):
                        pt = ps()
                        ptv = psb(pt)
                        k.tr(pt.v(ptv[0:C, 0:128]), kT[s][:, 0:C], ident_b[:, :])
                        k.tr(pt.v(ptv[0:C, 128:256]), vbf[s][:, 0:C], ident_b[:, :])
                        k.stt("dve", arg[s][0:C, 0:C], Gbc[0:C, h, 0:C], gsm[0:C, 5, h:h + 1], mneg_T(C), ALU.subtract, ALU.add)
                        k.tt("pool", qgT[s][:, 0:C], qT[s][:, 0:C], eGbc[:, h, 0:C], ALU.mult)
                        yield
                        k.actf(decT[s][0:C, 0:C], arg[s][0:C, 0:C], AF.Exp)
                        k.ts("dve", kd[s][0:C, :], pt.v(ptv[0:C, 0:128]), gsm[0:C, 9, h:h + 1], ALU.mult)
                        k.ts("dve", kbg[s][0:C, :], pt.v(ptv[0:C, 0:128]), gsm[0:C, 8, h:h + 1], ALU.mult)
                        k.ts("dve", vb[s][0:C, :], pt.v(ptv[0:C, 128:256]), gsm[0:C, 4, h:h + 1], ALU.mult)
                        yield
                        k.stt("dve", arg[s][0:C, 0:C], Gbc[0:C, h, 0:C], gsm[0:C, 5, h:h + 1], mpos_S(C), ALU.subtract, ALU.add)
                        yield
                        k.actf(dec[s][0:C, 0:C], arg[s][0:C, 0:C], AF.Exp, scale=-1.0)

                    def s_kk(h, s, st):
                        pk = ps()
                        k.mm(pk.v(pk.t[0:C, 0:C]), kT[s][:, 0:C], kT[s][:, 0:C])
                        k.mm(pk.v(pk.t[0:C, 128:128 + C]), kT[s][:, 0:C], qT[s][:, 0:C])
                        yield
                        k.stt("dve", Am[0][s][0:C, 0:C], pk.v(pk.t[0:C, 0:C]), gsm[0:C, 4, h:h + 1], dec[s][0:C, 0:C],
                              ALU.mult, ALU.mult)
                        k.tt("dve", qkT[s][0:C, 0:C], pk.v(pk.t[0:C, 128:128 + C]), decT[s][0:C, 0:C], ALU.mult)
                        yield
                        pB = ps()
                        pBv = psb(pB)
                        k.tr(pB.v(pBv[0:C, 0:C]), Am[0][s][0:C, 0:C], ident_b[0:C, 0:C])
                        yield
                        k.cp("act", Bm[0][s][0:C, 0:C], pB.v(pBv[0:C, 0:C]))
                        st["cd"] = 0

                    def s_level(lv):
                        def f(h, s, st):
                            cd = st["cd"]
                            k.tt("pool", Am[1][s][0:C, 0:C], Am[0][s][0:C, 0:C], msk[0:C, 2 * lv, 0:C], ALU.mult)
                            k.tt("dve", Bm[1][s][0:C, 0:C], Bm[0][s][0:C, 0:C], msk[0:C, 2 * lv + 1, 0:C], ALU.mult)
                            yield
                            if lv == 0:
                                k.tt("pool", Dm[0][s][0:C, 0:C], ident_b[0:C, 0:C], Am[1][s][0:C, 0:C], ALU.subtract)
                                k.tt("dve", Pm[0][s][0:C, 0:C], ident_b[0:C, 0:C], Bm[1][s][0:C, 0:C], ALU.subtract)
                                st["cd"] = 0
                                return
                            px = ps()
                            k.mm(px.v(px.t[0:C, 0:C]), Am[1][s][0:C, 0:C], Pm[cd][s][0:C, 0:C])
                            k.mm(px.v(px.t[0:C, 128:128 + C]), Bm[1][s][0:C, 0:C], Dm[cd][s][0:C, 0:C])
                            yield
                            k.cp("act", Xm[0][s][0:C, 0:C], px.v(px.t[0:C, 0:C]))
                            k.cp("act", Xm[1][s][0:C, 0:C], px.v(px.t[0:C, 128:128 + C]))
                            yield
                            py = ps()
                            k.mm(py.v(py.t[0:C, 0:C]), Dm[cd][s][0:C, 0:C], Xm[0][s][0:C, 0:C])
                            k.mm(py.v(py.t[0:C, 128:128 + C]), Pm[cd][s][0:C, 0:C], Xm[1][s][0:C, 0:C])
                            yield
                            k.stt("dve", Pm[1 - cd][s][0:C, 0:C], py.v(py.t[0:C, 0:C]), -1.0, Pm[cd][s][0:C, 0:C], ALU.mult, ALU.add)
                            k.stt("dve", Dm[1 - cd][s][0:C, 0:C], py.v(py.t[0:C, 128:128 + C]), -1.0, Dm[cd][s][0:C, 0:C],
                                  ALU.mult, ALU.add)
                            st["cd"] = 1 - cd
                        return f

                    def s_uw(h, s, st):
                        TTm = Pm[st["cd"]][s]
                        pu = ps()
                        k.mm(pu.v(pu.t[0:C, 0:128]), TTm[0:C, 0:C], vb[s][0:C, :])
                        k.mm(pu.v(pu.t[:, 128:128 + C]), kbg[s][0:C, :], TTm[0:C, 0:C])
                        yield
                        k.cp("act", ui[s][0:C, :], pu.v(pu.t[0:C, 0:128]))
                        k.cp("act", wT[s][:, 0:C], pu.v(pu.t[:, 128:128 + C]))

                    def s_scan(h, s, st):
                        pw = ps()
                        k.mm(pw.v(pw.t[0:C, 0:128]), wT[s][:, 0:C], Sb[:, h, :])
                        k.mm(pw.v(pw.t[0:C, 128:256]), qgT[s][:, 0:C], Sb[:, h, :], start=True, stop=False)
                        k.ts("pool", Sf[:, h, :], Sf[:, h, :], gsm[:, 7, h:h + 1], ALU.mult)
                        yield
                        k.stt("dve", ub[s][0:C, :], pw.v(pw.t[0:C, 0:128]), -1.0, ui[s][0:C, :], ALU.mult, ALU.add)
                        yield
                        k.mm(pw.v(pw.t[0:C, 128:256]), qkT[s][0:C, 0:C], ub[s][0:C, :], start=False, stop=True)
                        k.mm(pw.v(pw.t[:, 256:384]), kd[s][0:C, :], ub[s][0:C, :])
                        yield
                        k.tt("dve", Sf[:, h, :], pw.v(pw.t[:, 256:384]), Sf[:, h, :], ALU.add)
                        c0 = 8 + 2 * s
                        k.actf(onf[s][0:C, :], pw.v(pw.t[0:C, 128:256]), AF.Square, accum=col[0:C, c0:c0 + 1])
                        yield
                        k.cp("pool", Sb[:, h, :], Sf[:, h, :])
                        k.actf(col[0:C, c0 + 1:c0 + 2], col[0:C, c0:c0 + 1], AF.Ln, bias=EPS, scale=1.0 / 128)
                        yield
                        k.actf(col[0:C, c0 + 1:c0 + 2], col[0:C, c0 + 1:c0 + 2], AF.Exp, scale=-0.5)
                        yield
                        k.stt("dve", onf[s][0:C, :], pw.v(pw.t[0:C, 128:256]), col[0:C, c0 + 1:c0 + 2], onb[0:C, :], ALU.mult, ALU.mult)
                        yield
                        k.tt("pool", og[0:C, h * 128:(h + 1) * 128], onf[s][0:C, :], zs[0:C, h * 128:(h + 1) * 128], ALU.mult)

                    stages = [s_conv, s_norm(0), s_norm(1), s_tok, s_kk]
                    stages += [s_level(lv) for lv in range(nlev)]
                    stages += [s_uw, s_scan]
                    def head_gen(h, s_, st_):
                        for stg_f in stages:
                            yield from stg_f(h, s_, st_)
                            yield

                    LAG = LAGS
                    for g0 in range(0, H, HG):
                        hs = list(range(g0, min(H, g0 + HG)))
                        gens = [head_gen(h, h - g0, {}) for h in hs]
                        alive = set(range(len(gens)))
                        rnd = 0
                        while alive:
                            for jx in sorted(alive):
                                if rnd >= LAG[jx]:
                                    try:
                                        next(gens[jx])
                                    except StopIteration:
                                        alive.discard(jx)
                            rnd += 1
                    k.cp("pool", cst32[:, :, :], ue[:, :, C:C + 3])
                    k.cp("pool", ue[:, :, 0:3], cst32[:, :, :])

                if DBG_STAGE < 3:
                    k.barrier()
                    k.stack = old
                    return
                load_mod(l, 0, 128)
                k.memset("pool", Sf[:, :, :], 0.0)
                k.memset("pool", Sb[:, :, :], 0.0)
                k.memset("pool", uext[:, :, 0:3], 0.0)
                for t in range(NT if DBG_STAGE >= 4 else 0):
                    x = xt[t % 2]
                    k.dma(x[:, :], Dx[t].v(src_p[t * 128:(t + 1) * 128, :]))
                    gdn_tile(x, 128, uext)
                    out_proj_post(x, 128, Dx[t].v(dst_p[t * 128:(t + 1) * 128, :]))
                k.dma(Dout.v(st_p[j].rearrange("h k v -> k h v")), Sf[:, :, :])
                for jj in range(3):
                    k.dma(Dout.v(conv_p[j, jj].rearrange("(c p) -> p c", p=128)), cst32[:, :, jj], slow=True)
                for s in range(0 if DBG_NOSAMPLE else NSB):
                    load_mod(l, 1 + s, TS)
                    k.dma(Sf[:, :, :], Din.v(st_in[j, s].rearrange("h k v -> k h v")))
                    k.cp("pool", Sb[:, :, :], Sf[:, :, :])
                    k.dma(cst32[:, :, :], Din.v(cc_in[j, s]))
                    k.cp("pool", uext_s[:, :, 0:3], cst32[:, :, :])
                    x = xt[s % 2]
                    k.dma(x[0:TS, :], Dxs[s].v(src_s[s]))
                    gdn_tile(x, TS, uext_s)
                    out_proj_post(x, TS, Dxs[s].v(dst_s[s]))
                    k.dma(Dout.v(st_s[j, s].rearrange("h k v -> k h v")), Sf[:, :, :])
                    for jj in range(3):
                        k.dma(Dout.v(conv_s[j, s, jj].rearrange("(c p) -> p c", p=128)), cst32[:, :, jj], slow=True)
                k.barrier()
                k.stack = old

        def diff_layer(l, j, src_p, dst_p, src_s, dst_s):
            lam_init = 0.8 - 0.6 * math.exp(-0.3 * l)
            with contextlib.ExitStack() as st2:
                old = k.stack
                k.stack = st2
                QS = min(256, T)
                NQ = QS // 128
                lq = k.sb("lq", [128, 4, 64], F32, dma=True)
                lamc = k.sb("lamc", [128, 4], F32)
                snb = k.sb("snb", [128, 128], F32, dma=True)
                cs = k.sb("cs", [128, 2, 32], F32, dma=True)
                pr = k.sb("pr", [128, 2, D], F32, dma=True)
                vz = k.sb("vz", [128, 2, D], F32, dma=True)
                qkb = k.sb("qkb", [128, 2, D], BF16)
                vb16 = k.sb("vb16", [128, D], BF16, dma=True)
                qkT = k.sb("qkT_t", [128, 2, H, 128], BF16, dma=True)
                KTh = k.sb("KTh", [128, max(T, PAST + TS)], BF16, dma=True)
                Vh = k.sb("Vh", [128, max(NT, NPT + 1), 132], BF16, dma=True)
                QTh = k.sb("QTh", [128, QS], BF16, dma=True)
                PT = [k.sb("PT%d" % i, [128, QS], BF16) for i in range(4)]
                osb = [k.sb("osb%d" % i, [128, 2, 132], F32) for i in range(4)]
                odf = [k.sb("odf%d" % i, [128, 128], F32, dma=True) for i in range(2)]
                o_t = k.sb("o_t", [128, D], F32, dma=True)
                onf = k.sb("onf2", [128, 128], F32)

                with contextlib.ExitStack() as st3:
                    k.stack = st3
                    stg_l = [k.sb("stgl%d" % i, [128, 1028], F32, dma=True) for i in range(2)]
                    load_weight(W_in, w_in_diff[j], DIN, stg_l)
                    load_weight(W_out, w_out_diff[j], D, stg_l)
                    k.barrier()
                    k.stack = st2
                k.dma(lq[:, :, :], Din.v(lamv[j:j + 1].to_broadcast([128, 4, 64])))
                k.tt("dve", lq[:, 0, :], lq[:, 0, :], lq[:, 1, :], ALU.mult)
                k.tt("dve", lq[:, 2, :], lq[:, 2, :], lq[:, 3, :], ALU.mult)
                k.rsum(lamc[:, 0:1], lq[:, 0, :])
                k.rsum(lamc[:, 1:2], lq[:, 2, :])
                k.actf(lamc[:, 0:2], lamc[:, 0:2], AF.Exp)
                k.tt("dve", lamc[:, 2:3], lamc[:, 0:1], lamc[:, 1:2], ALU.subtract)
                k.ts("dve", lamc[:, 2:3], lamc[:, 2:3], -1.0, ALU.mult, -lam_init, ALU.add)
                k.dma(snb[:, :], Din.v(subln[j:j + 1, :].to_broadcast([128, 128])))
                k.ts("dve", snb[:, :], snb[:, :], 1.0 - lam_init, ALU.mult)

                def proj_tile(x, C, rope_ap, kdst, vdst):
                    norm_mod_T(x, C)
                    k.dma(cs[0:C, :, :], Din.v(rope_ap.rearrange("a t d -> t a d")))
                    for blk in range(8):
                        pb = ps()
                        for kc in range(KC):
                            k.mm(psf(pb, C, 512), hT[:, kc, 0:C], W_in[:, kc, blk * 512:(blk + 1) * 512],
                                 start=(kc == 0), stop=(kc == KC - 1))
                        if blk < 4:
                            k.cp("act", pr[0:C, blk // 2, (blk % 2) * 512:(blk % 2 + 1) * 512], psf(pb, C, 512))
                        elif blk < 6:
                            k.cp("act", vz[0:C, 0, (blk - 4) * 512:(blk - 3) * 512], psf(pb, C, 512))
                        else:
                            k.actf(vz[0:C, 1, (blk - 6) * 512:(blk - 5) * 512], psf(pb, C, 512), AF.Silu)
                    cosb = cs.v(cs.t[0:C, 0, :].unsqueeze(1).to_broadcast([C, 16, 32]))
                    sinb = cs.v(cs.t[0:C, 1, :].unsqueeze(1).to_broadcast([C, 16, 32]))
                    rtA = lambda i: o_t.v(o_t.t[0:C, :].rearrange("p (a g d) -> p a g d", a=2, g=16)[:, i])
                    rtB = lambda i: t1.v(t1.t[0:C, :].rearrange("p (a g d) -> p a g d", a=2, g=16)[:, i])
                    for i2 in range(2):
                        xv = pr.t[0:C, i2, :].rearrange("p (g two d) -> p g two d", g=16, two=2)
                        x1 = pr.v(xv[:, :, 0, :])
                        x2 = pr.v(xv[:, :, 1, :])
                        e1, e2 = ("dve", "pool") if i2 == 0 else ("pool", "dve")
                        k.tt(e1, rtA(0), x1, cosb, ALU.mult)
                        k.tt(e1, rtA(1), x2, sinb, ALU.mult)
                        k.tt(e2, rtB(0), x2, cosb, ALU.mult)
                        k.tt(e2, rtB(1), x1, sinb, ALU.mult)
                        k.tt(e1, x1, rtA(0), rtA(1), ALU.subtract)
                        k.tt(e2, x2, rtB(0), rtB(1), ALU.add)
                    k.dma(kdst, pr[0:C, 1, :])
                    k.dma(vdst, vz[0:C, 0, :])
                    k.ts("dve", qkb[0:C, 0, :], pr[0:C, 0, :], 0.125, ALU.mult)
                    k.cp("pool", qkb[0:C, 1, :], pr[0:C, 1, :])
                    k.cp("pool", vb16[0:C, :], vz[0:C, 0, :])
                    for i2 in range(2):
                        pb = ps()
                        pv = psb(pb)
                        for h in range(H):
                            k.tr(pb.v(pv[:, h * 128:h * 128 + C]), qkb[0:C, i2, h * 128:(h + 1) * 128], ident_b[0:C, 0:C])
                        k.cp("act", qkT[:, i2, :, 0:C], pb.v(pv.rearrange("p (h c) -> p h c", h=H)[:, :, 0:C]))

                def sub_out(C, ocomb, h):
                    k.actf(onf[0:C, :], ocomb, AF.Square, accum=col[0:C, 8:9])
                    rstd_from_ss(col[0:C, 8:9], C, 128, col[0:C, 9:10])
                    k.stt("dve", onf[0:C, :], ocomb, col[0:C, 9:10], snb[0:C, :], ALU.mult, ALU.mult)

                def combine(C, ob, dstv):
                    k.recip(ob[0:C, :, 129:130], ob[0:C, :, 128:129])
                    k.ts("dve", ob[0:C, 1, 129:130], ob[0:C, 1, 129:130], lamc[0:C, 2:3], ALU.mult)
                    k.ts("dve", ob[0:C, 0, 0:128], ob[0:C, 0, 0:128], ob[0:C, 0, 129:130], ALU.mult)
                    k.stt("dve", dstv, ob[0:C, 1, 0:128], ob[0:C, 1, 129:130], ob[0:C, 0, 0:128], ALU.mult, ALU.add)

                if DBG_D <= 1:
                    k.barrier()
                    k.stack = old
                    return
                load_mod(l, 0, 128)
                for t in range(NT):
                    x = xt[t % 2]
                    k.dma(x[:, :], Dx[t].v(src_p[t * 128:(t + 1) * 128, :]))
                    proj_tile(x, 128, rope_p[:, t * 128:(t + 1) * 128, :],
                              Dout.v(k_p[j, t * 128:(t + 1) * 128, :]), Dout.v(v_p[j, t * 128:(t + 1) * 128, :]))
                    k.dma(Dq[0].v(qT_d[:, :, t * 128:(t + 1) * 128].rearrange("h p c -> p h c")), qkT[:, 0, :, :])
                    k.dma(Dq[0].v(kT_d[:, :, t * 128:(t + 1) * 128].rearrange("h p c -> p h c")), qkT[:, 1, :, :])
                    k.dma(Dq[0].v(vb_d[:, t * 128:(t + 1) * 128, :].rearrange("h p d -> p h d")),
                          vb16.v(vb16.t[:, :].rearrange("p (h d) -> p h d", h=H)))
                    k.dma(Dz[t].v(z_d[t * 128:(t + 1) * 128, :]), vz[:, 1, :])
                if DBG_D <= 2:
                    k.barrier()
                    k.stack = old
                    return
                k.memset("pool", Vh[:, :, 128:129], 1.0)
                for h in range(H):
                    k.dma(KTh[:, 0:T], Dq[0].v(kT_d[h]))
                    k.dma(Vh[:, 0:NT, 0:128], Dq[0].v(vb_d[h].rearrange("(t p) d -> p t d", p=128)))
                    for qs in range(T // QS):
                        k.dma(QTh[:, :], Dq[0].v(qT_d[h, :, qs * QS:(qs + 1) * QS]))
                        nkv = (qs + 1) * NQ
                        pso = [[ps(hold=True) for _ in range(NQ)] for _c in range(2)]

                        def emit_qk(jt_):
                            r0_ = max(0, jt_ - qs * NQ)
                            pp_ = []
                            for c_ in range(2):
                                p_ = ps()
                                k.mm(p_.v(p_.t[:, r0_ * 128:QS]), KTh[c_ * 64:(c_ + 1) * 64, jt_ * 128:(jt_ + 1) * 128],
                                     QTh[c_ * 64:(c_ + 1) * 64, r0_ * 128:QS])
                                pp_.append(p_)
                            return pp_
                        pend = emit_qk(0)
                        for jt in range(nkv):
                            r0 = max(0, jt - qs * NQ)
                            pst2 = pend
                            if jt + 1 < nkv:
                                pend = emit_qk(jt + 1)
                            for c in range(2):
                                pst = pst2[c]
                                pt_ = PT[(2 * jt + c) % 4]
                                k.actf(pt_[:, r0 * 128:QS], pst.v(pst.t[:, r0 * 128:QS]), AF.Exp)
                                if jt >= qs * NQ:
                                    k.memset("pool", pt_[64:128, r0 * 128:r0 * 128 + 64], 0.0)
                                for r in range(r0, NQ):
                                    k.mm(pso[c][r].v(pso[c][r].t[:, 0:129]), pt_[:, r * 128:(r + 1) * 128],
                                         Vh[:, jt, 0:129], start=(jt == 0), stop=(jt == qs * NQ + r))
                        for c in range(2):
                            for r in range(NQ):
                                k.cp("act", osb[r][:, c, 0:129], pso[c][r].v(pso[c][r].t[:, 0:129]))
                            unhold(pso[c])
                        for r in range(NQ):
                            tq = qs * NQ + r
                            sl = (h * (T // 128) + tq) % 2
                            combine(128, osb[r], odf[sl][:, :])
                            k.dma(Do[tq].v(o_d[tq * 128:(tq + 1) * 128, h * 128:(h + 1) * 128]), odf[sl][:, :])
                if DBG_D <= 3:
                    k.barrier()
                    k.stack = old
                    return
                for t in range(NT):
                    x = xt[t % 2]
                    k.dma(x[:, :], Dx[t].v(src_p[t * 128:(t + 1) * 128, :]))
                    k.dma(o_t[:, :], Do[t].v(o_d[t * 128:(t + 1) * 128, :]))
                    k.dma(vz[:, 1, :], Dz[t].v(z_d[t * 128:(t + 1) * 128, :]))
                    for h in range(H):
                        sub_out(128, o_t[:, h * 128:(h + 1) * 128], h)
                        k.tt("pool", og[:, h * 128:(h + 1) * 128], onf[:, :], vz[:, 1, h * 128:(h + 1) * 128], ALU.mult)
                    out_proj_post(x, 128, Dx[t].v(dst_p[t * 128:(t + 1) * 128, :]))
                if DBG_D <= 4:
                    k.barrier()
                    k.stack = old
                    return
                for s in range(0 if DBG_NOSAMPLE else NSB):
                    C = TS
                    load_mod(l, 1 + s, C)
                    x = xt[s % 2]
                    k.dma(x[0:C, :], Dxs[s].v(src_s[s]))
                    proj_tile(x, C, rope_s, Dout.v(k_s[j, s]), Dout.v(v_s[j, s]))
                    k.memset("pool", Vh[:, :, 128:129], 1.0)
                    for h in range(H):
                        for c0 in range(0, PAST, 512):
                            k.dma(pr[:, 0, 0:512], Din.v(ck_in[j, s, h, :, c0:c0 + 512]))
                            k.cp("dve", KTh[:, c0:c0 + 512], pr[:, 0, 0:512])
                            k.dma(pr.v(pr.t[:, 1, 0:512].rearrange("p (t d) -> p t d", t=4)),
                                  Din.v(cv_in[j, s, c0:c0 + 512, h, :].rearrange("(t p) d -> p t d", p=128)))
                            k.cp("pool", Vh[:, c0 // 128:c0 // 128 + 4, 0:128],
                                 pr.v(pr.t[:, 1, 0:512].rearrange("p (t d) -> p t d", t=4)))
                        k.cp("dve", KTh[:, PAST:PAST + C], qkT[:, 1, h, 0:C])
                        k.cp("pool", Vh[0:C, NPT, 0:128], vb16[0:C, h * 128:(h + 1) * 128])
                        for c in range(2):
                            pso = ps(hold=True)
                            def emit_qk_s(jt_):
                                rows_ = 128 if jt_ < NPT else C
                                p_ = ps()
                                k.mm(p_.v(p_.t[0:rows_, 0:C]), KTh[c * 64:(c + 1) * 64, jt_ * 128:jt_ * 128 + rows_],
                                     qkT[c * 64:(c + 1) * 64, 0, h, 0:C])
                                return p_
                            pend = emit_qk_s(0)
                            for jt in range(NPT + 1):
                                rows = 128 if jt < NPT else C
                                pst = pend
                                if jt + 1 < NPT + 1:
                                    pend = emit_qk_s(jt + 1)
                                pt_ = PT[jt % 4]
                                k.actf(pt_[0:rows, 0:C], pst.v(pst.t[0:rows, 0:C]), AF.Exp)
                                k.mm(pso.v(pso.t[0:C, 0:129]), pt_[0:rows, 0:C], Vh[0:rows, jt, 0:129],
                                     start=(jt == 0), stop=(jt == NPT))
                            k.cp("act", osb[h % 4][0:C, c, 0:129], pso.v(pso.t[0:C, 0:129]))
                            unhold([pso])
                        combine(C, osb[h % 4], o_t[0:C, h * 128:(h + 1) * 128])
                    for h in range(H):
                        sub_out(C, o_t[0:C, h * 128:(h + 1) * 128], h)
                        k.tt("pool", og[0:C, h * 128:(h + 1) * 128], onf[0:C, :], vz[0:C, 1, h * 128:(h + 1) * 128], ALU.mult)
                    out_proj_post(x, C, Dxs[s].v(dst_s[s]))
                k.barrier()
                k.stack = old

        for l in range(depth if DBG_STAGE >= 2 else 0):
            src_p = xp if l == 0 else xa
            dst_p = y_p if l == depth - 1 else xa
            src_s = xs if l == 0 else xsa
            dst_s = y_s if l == depth - 1 else xsa
            if l % 2 == 0:
                gdn_layer(l, l // 2, src_p, dst_p, src_s, dst_s)
            else:
                diff_layer(l, l // 2, src_p, dst_p, src_s, dst_s)
        k.barrier()
        DBG['k'] = k
        print("instructions emitted:", k.nins)
    return nc


def _consts():
    i = np.arange(128)
    ident = np.eye(128, dtype=np.float32)
    U = (i[:, None] <= i[None, :]).astype(np.float32)
    mnegT = np.where(i[None, :] >= i[:, None], 0.0, NEG).astype(np.float32)
    mposS = np.where(i[None, :] < i[:, None], 0.0, -NEG).astype(np.float32)
    ones = np.ones((128, 128), np.float32)
    return np.stack([ident, U, mnegT, mposS, ones]).astype(np.float32)


def _lvmasks():
    i = np.arange(128)
    out = []
    sz = 1
    while sz < 128:
        m = ((i[:, None] // (2 * sz)) == (i[None, :] // (2 * sz))) & ((i[:, None] % (2 * sz)) >= sz) & ((i[None, :] % (2 * sz)) < sz)
        out.append(m.astype(np.float32))
        out.append(m.T.astype(np.float32))
        sz *= 2
    return np.stack(out).astype(np.float32)


def _rope_tab(pos):
    half = 32
    inv = (1.0 / (np.float32(10000.0) ** (np.arange(half, dtype=np.float32) / np.float32(half)))).astype(np.float32)
    ang = pos.astype(np.float32)[:, None] * inv[None, :]
    return np.stack([np.cos(ang), np.sin(ang)]).astype(np.float32)


def _wl(w):
    L, _, N = w.shape
    return np.ascontiguousarray(w.reshape(L, KC, 128, N).transpose(0, 2, 1, 3))


_PROG_CACHE = {}


def run(inputs, n_cores, T, PAST, NSB, TS, prompt_of_core, depth=DEPTH):
    key = (T, PAST, NSB, TS, depth)
    if key not in _PROG_CACHE:
        _PROG_CACHE[key] = build_program(T, PAST, NSB, TS, depth)
    nc = _PROG_CACHE[key]
    f = lambda a: np.ascontiguousarray(np.asarray(a, dtype=np.float32))
    shared = {
        "norm_pre": f(inputs["norm_pre"]), "norm_post": f(inputs["norm_post"]),
        "w_ada": _wl(f(inputs["w_ada"])), "b_ada": f(inputs["b_ada"]),
        "w_in_gdn": _wl(f(inputs["w_in_gdn"])),
        "conv_w": np.ascontiguousarray(f(inputs["conv_gdn"]).reshape(2, 4, 24, 128).transpose(0, 3, 2, 1)),
        "a_log": f(inputs["a_log_gdn"]), "dt_bias": f(inputs["dt_bias_gdn"]), "onorm": f(inputs["onorm_gdn"]),
        "w_out_gdn": _wl(f(inputs["w_out_gdn"])), "w_in_diff": _wl(f(inputs["w_in_diff"])),
        "lamv": np.ascontiguousarray(np.stack([f(inputs["lam_q1"]), f(inputs["lam_k1"]), f(inputs["lam_q2"]),
                                               f(inputs["lam_k2"])], axis=1)),
        "subln": f(inputs["subln_diff"]), "w_out_diff": _wl(f(inputs["w_out_diff"])),
        "cst": _consts(), "lvmask": _lvmasks(), "rope_p": _rope_tab(np.arange(T)), "rope_s": _rope_tab(PAST + np.arange(TS)),
    }
    xp = f(inputs["x_prompt"]); xs = f(inputs["x_sample"])
    cp = f(inputs["c_prompt"]); csm = f(inputs["c_sample"])
    stg = f(inputs["state_gdn"]); cc = f(inputs["cache_conv"]); ck = f(inputs["cache_k"]); cv = f(inputs["cache_v"])
    in_maps = []
    for c in range(n_cores):
        pb = prompt_of_core[c]
        sb = list(range(c * NSB, (c + 1) * NSB))
        cvec = np.stack([cp[pb]] + [csm[s] for s in sb], axis=-1)
        m = dict(shared)
        m["xp"] = xp[pb]
        m["xs"] = np.ascontiguousarray(xs[sb])
        m["c3"] = np.ascontiguousarray(cvec.reshape(KC, 128, 1 + NSB).transpose(1, 0, 2))
        m["st_in"] = np.ascontiguousarray(stg[:, sb])
        m["cc_in"] = np.ascontiguousarray(cc[:, sb].reshape(cc.shape[0], NSB, 3, 24, 128).transpose(0, 1, 4, 3, 2))
        m["ck_in"] = np.ascontiguousarray(ck[:, sb].transpose(0, 1, 3, 4, 2))
        m["cv_in"] = np.ascontiguousarray(cv[:, sb])
        in_maps.append(m)
    res = run_bass_kernel_spmd(nc, in_maps, core_ids=list(range(n_cores)))
    return res.results


def kernel(**inputs):
    B, T, _ = inputs["x_prompt"].shape
    SBT, TS, _ = inputs["x_sample"].shape
    PAST = inputs["cache_k"].shape[2]
    n_cores = 8
    NSB = SBT // n_cores
    prompt_of_core = [c % B for c in range(n_cores)]
    r = run(inputs, n_cores, T, PAST, NSB, TS, prompt_of_core)
    y_p = np.stack([r[b]["y_p"] for b in range(B)])
    y_s = np.concatenate([r[c]["y_s"] for c in range(n_cores)], axis=0)
    st_p = np.stack([r[b]["st_p"] for b in range(B)], axis=1)
    conv_p = np.stack([r[b]["conv_p"] for b in range(B)], axis=1)
    k_p = np.stack([r[b]["k_p"] for b in range(B)], axis=1).reshape(2, B, T, H, 128)
    v_p = np.stack([r[b]["v_p"] for b in range(B)], axis=1).reshape(2, B, T, H, 128)
    st_s = np.concatenate([r[c]["st_s"] for c in range(n_cores)], axis=1)
    conv_s = np.concatenate([r[c]["conv_s"] for c in range(n_cores)], axis=1)
    k_s = np.concatenate([r[c]["k_s"] for c in range(n_cores)], axis=1).reshape(2, SBT, TS, H, 128)
    v_s = np.concatenate([r[c]["v_s"] for c in range(n_cores)], axis=1).reshape(2, SBT, TS, H, 128)
    return tuple(np.ascontiguousarray(a.astype(np.float32)) for a in
                 (y_p, y_s, st_p, conv_p, k_p, v_p, st_s, conv_s, k_s, v_s))
```

```python
import contextlib
import math
import numpy as np
import concourse.bass as bass
import concourse.mybir as mybir
from concourse.bass_utils import run_bass_kernel_spmd

F32 = mybir.dt.float32
BF16 = mybir.dt.bfloat16
AF = mybir.ActivationFunctionType
ALU = mybir.AluOpType
AX = mybir.AxisListType

D = 1024
KC = 8
H = 8
DEPTH = 4
GIN = 4112
DIN = 4096
EPS = 1e-6
NEG = -1.0e9
import os
DBG_NOSAMPLE = bool(os.environ.get('DBG_NOSAMPLE'))
DBG = {}
EMBED_WAIT = not os.environ.get('NO_EMBED')
DBG_D = int(os.environ.get('DBG_D', '99'))
DBG_CUT = int(os.environ.get('DBG_CUT', '99'))
DBG_STAGE = int(os.environ.get('DBG_STAGE', '99'))


class Ctr:
    LIM = 30000

    def __init__(self, k, name):
        self.k = k
        self.name = name
        self.sems = []
        self.n = 0

    def sem(self, epoch):
        while len(self.sems) <= epoch:
            s = self.k.perm.enter_context(self.k.nc.semaphore("%s_%d" % (self.name, len(self.sems))))
            self.sems.append(s)
        return self.sems[epoch]

    def bump(self, inc):
        if (self.n % self.LIM) + inc > self.LIM:
            self.n = (self.n // self.LIM + 1) * self.LIM
        e = self.n // self.LIM
        self.n += inc
        return self.sem(e), self.n

    def target(self, val):
        e = (val - 1) // self.LIM
        return self.sem(e), val - e * self.LIM


class V:
    __slots__ = ("b", "ap")

    def __init__(self, b, ap):
        self.b = b
        self.ap = ap


class Buf:
    def __init__(self, t, name, dma=False, k=None):
        self.t = t
        self.name = name
        self.w = None
        self.r = {}
        self.psum = False
        self.ctr = Ctr(k, "d_" + name) if dma else None

    def __getitem__(self, key):
        return V(self, self.t[key])

    def v(self, ap):
        return V(self, ap)


class Eng:
    def __init__(self, k, name, e):
        self.name = name
        self.e = e
        self.ctr = Ctr(k, "e_" + name)
        self.waited = {}


class K:
    def __init__(self, nc, stack):
        self.nc = nc
        self.stack = stack
        self.perm = stack
        self.held = set()
        self.nname = 0
        self.pe = Eng(self, "pe", nc.tensor)
        self.act = Eng(self, "act", nc.scalar)
        self.dve = Eng(self, "dve", nc.vector)
        self.pool = Eng(self, "pool", nc.gpsimd)
        self.sp = Eng(self, "sp", nc.sync)
        self.engs = [self.pe, self.act, self.dve, self.pool, self.sp]
        self.dma_ctrs = []
        self.nps = 0
        self.nins = 0

    def sb(self, name, shape, dt, dma=False):
        self.nname += 1
        name = "%s_%d" % (name, self.nname)
        t = self.stack.enter_context(self.nc.sbuf_tensor(name, list(shape), dt))
        b = Buf(t, name, dma, self)
        if dma:
            self.dma_ctrs.append(b.ctr)
        return b

    def dram(self, ap, name):
        return Buf(ap, name, False, self)

    def _wait(self, eng, ctr, val):
        if val <= 0:
            return
        if eng.waited.get(ctr, 0) >= val:
            return
        sem, v = ctr.target(val)
        eng.e.wait_ge(sem, v)
        eng.waited[ctr] = val

    def op(self, eng, fn, outs, ins, dma_buf=None):
        deps = []
        for v in ins:
            if v.b.w is not None:
                deps.append(v.b.w)
            if v.b.psum:
                for c, val in v.b.r.items():
                    if c is not eng.ctr:
                        deps.append((c, val))
        own = None
        for v in outs:
            if v.b.w is not None and v.b.w[0] is not own:
                deps.append(v.b.w)
            for c, val in v.b.r.items():
                if c is not own:
                    deps.append((c, val))
        need = {}
        for c, val in deps:
            if eng is self.pe and c is self.pe.ctr:
                continue
            if val > 0 and eng.waited.get(c, 0) < val and need.get(c, 0) < val:
                need[c] = val
        if dma_buf is not None and dma_buf.ctr.n > 0 and eng.waited.get(dma_buf.ctr, 0) < dma_buf.ctr.n:
            need[dma_buf.ctr] = dma_buf.ctr.n
        need = list(need.items())
        embed = None
        if EMBED_WAIT and dma_buf is None and need:
            embed = need.pop()
        for c, val in need:
            self._wait(eng, c, val)
        ins_obj = fn()
        if embed is not None:
            sem, v = embed[0].target(embed[1])
            ins_obj._wait_ge(sem, v)
            eng.waited[embed[0]] = embed[1]
        if dma_buf is not None:
            ctr = dma_buf.ctr
            sem, val = ctr.bump(16)
            ins_obj.then_inc(sem, 16)
        else:
            ctr = eng.ctr
            sem, val = ctr.bump(1)
            ins_obj.then_inc(sem, 1)
        for v in ins:
            v.b.r[ctr] = val
        for v in outs:
            v.b.w = (ctr, val)
            v.b.r = {}
        self.nins += 1

    def barrier(self):
        ctrs = [e.ctr for e in self.engs[:4]] + self.dma_ctrs
        for e in self.engs:
            for c in ctrs:
                if c is e.ctr and e is self.pe:
                    continue
                self._wait(e, c, c.n)

    def mm(self, out, lhsT, rhs, start=True, stop=True):
        self.op(self.pe, lambda: self.nc.tensor.matmul(out.ap, lhsT=lhsT.ap, rhs=rhs.ap, start=start, stop=stop),
                [out], [lhsT, rhs])

    def tr(self, out, in_, ident):
        self.op(self.pe, lambda: self.nc.tensor.transpose(out.ap, in_.ap, ident.ap), [out], [in_, ident])

    def actf(self, out, in_, func, bias=0.0, scale=1.0, accum=None):
        ins = [in_]
        outs = [out]
        kw = {}
        if isinstance(bias, V):
            ins.append(bias)
            kw["bias"] = bias.ap
        else:
            kw["bias"] = float(bias)
        if isinstance(scale, V):
            ins.append(scale)
            kw["scale"] = scale.ap
        else:
            kw["scale"] = float(scale)
        if accum is not None:
            outs.append(accum)
            kw["accum_out"] = accum.ap
        self.op(self.act, lambda: self.nc.scalar.activation(out=out.ap, in_=in_.ap, func=func, **kw), outs, ins)

    def _e(self, eng):
        return {"dve": self.dve, "pool": self.pool, "act": self.act}[eng]

    def tt(self, eng, out, in0, in1, op):
        e = self._e(eng)
        self.op(e, lambda: e.e.tensor_tensor(out=out.ap, in0=in0.ap, in1=in1.ap, op=op), [out], [in0, in1])

    def ts(self, eng, out, in0, s1, op0, s2=None, op1=None):
        e = self._e(eng)
        ins = [in0]
        a1 = s1
        a2 = s2
        if isinstance(s1, V):
            ins.append(s1)
            a1 = s1.ap
        if isinstance(s2, V):
            ins.append(s2)
            a2 = s2.ap
        if op1 is None:
            self.op(e, lambda: e.e.tensor_scalar(out=out.ap, in0=in0.ap, scalar1=a1, scalar2=None, op0=op0),
                    [out], ins)
        else:
            self.op(e, lambda: e.e.tensor_scalar(out=out.ap, in0=in0.ap, scalar1=a1, scalar2=a2, op0=op0, op1=op1),
                    [out], ins)

    def stt(self, eng, out, in0, s, in1, op0, op1):
        e = self._e(eng)
        ins = [in0, in1]
        a = s
        if isinstance(s, V):
            ins.append(s)
            a = s.ap
        self.op(e, lambda: e.e.scalar_tensor_tensor(out=out.ap, in0=in0.ap, scalar=a, in1=in1.ap, op0=op0, op1=op1),
                [out], ins)

    def cp(self, eng, out, in_):
        if eng == "act":
            self.op(self.act, lambda: self.nc.scalar.copy(out=out.ap, in_=in_.ap), [out], [in_])
        elif eng == "dve" and in_.b.name.startswith("ps"):
            self.ts("dve", out, in_, 1.0, ALU.mult)
        else:
            e = self._e(eng)
            self.op(e, lambda: e.e.tensor_copy(out=out.ap, in_=in_.ap), [out], [in_])

    def memset(self, eng, out, val):
        e = self._e(eng)
        self.op(e, lambda: e.e.memset(out.ap, val), [out], [])

    def recip(self, out, in_):
        self.op(self.dve, lambda: self.nc.vector.reciprocal(out=out.ap, in_=in_.ap), [out], [in_])

    def rsum(self, out, in_):
        self.op(self.dve, lambda: self.nc.vector.reduce_sum(out=out.ap, in_=in_.ap, axis=AX.X), [out], [in_])

    def dma(self, out, in_, q=None, slow=False):
        q = q or self.sp
        owner = out.b if out.b.ctr is not None else in_.b
        assert owner.ctr is not None, (out.b.name, in_.b.name)
        kw = {"allow_slow_non_contiguous": True} if slow else {}
        self.op(q, lambda: q.e.dma_start(out=out.ap, in_=in_.ap, **kw), [out], [in_], dma_buf=owner)


def build_program(T, PAST, NSB, TS=16, depth=DEPTH):
    nc = bass.Bass("TRN2", target_bir_lowering=False)
    NT = T // 128
    NPT = PAST // 128
    n_gdn = (depth + 1) // 2
    n_diff = depth // 2

    def din(name, shape, dt=F32):
        return nc.dram_tensor(name, list(shape), dt, kind="ExternalInput").ap()

    def dout(name, shape, dt=F32):
        return nc.dram_tensor(name, list(shape), dt, kind="ExternalOutput").ap()

    def dscr(name, shape, dt=F32):
        return nc.dram_tensor(name, list(shape), dt).ap()

    xp = din("xp", [T, D])
    xs = din("xs", [NSB, TS, D])
    c3 = din("c3", [128, KC, 1 + NSB])
    st_in = din("st_in", [2, NSB, H, 128, 128])
    cc_in = din("cc_in", [2, NSB, 128, 24, 3])
    ck_in = din("ck_in", [2, NSB, H, 128, PAST])
    cv_in = din("cv_in", [2, NSB, PAST, H, 128])
    norm_pre = din("norm_pre", [DEPTH, D])
    norm_post = din("norm_post", [DEPTH, D])
    w_ada = din("w_ada", [DEPTH, 128, KC, 3 * D])
    b_ada = din("b_ada", [DEPTH, 3 * D])
    w_in_gdn = din("w_in_gdn", [2, 128, KC, GIN])
    conv_w = din("conv_w", [2, 128, 24, 4])
    a_log = din("a_log", [2, H])
    dt_bias = din("dt_bias", [2, H])
    onorm = din("onorm", [2, 128])
    w_out_gdn = din("w_out_gdn", [2, 128, KC, D])
    w_in_diff = din("w_in_diff", [2, 128, KC, DIN])
    lamv = din("lamv", [2, 4, 64])
    subln = din("subln", [2, 128])
    w_out_diff = din("w_out_diff", [2, 128, KC, D])
    cst = din("cst", [5, 128, 128])
    lvmask = din("lvmask", [14, 128, 128])
    rope_p = din("rope_p", [2, T, 32])
    rope_s = din("rope_s", [2, TS, 32])
    y_p = dout("y_p", [T, D])
    y_s = dout("y_s", [NSB, TS, D])
    st_p = dout("st_p", [n_gdn, H, 128, 128])
    conv_p = dout("conv_p", [n_gdn, 3, 3072])
    k_p = dout("k_p", [max(n_diff, 1), T, D])
    v_p = dout("v_p", [max(n_diff, 1), T, D])
    st_s = dout("st_s", [n_gdn, NSB, H, 128, 128])
    conv_s = dout("conv_s", [n_gdn, NSB, 3, 3072])
    k_s = dout("k_s", [max(n_diff, 1), NSB, TS, D])
    v_s = dout("v_s", [max(n_diff, 1), NSB, TS, D])
    xa = dscr("xa", [T, D])
    xsa = dscr("xsa", [NSB, TS, D])
    ada_d = dscr("ada_d", [DEPTH, 1 + NSB, 3 * D])
    qT_d = dscr("qT_d", [H, 128, T], BF16)
    kT_d = dscr("kT_d", [H, 128, T], BF16)
    vb_d = dscr("vb_d", [H, T, 128], BF16)
    z_d = dscr("z_d", [T, D])
    o_d = dscr("o_d", [T, D])

    with contextlib.ExitStack() as stack:
        k = K(nc, stack)
        Dx = [k.dram(None, "dx%d" % t) for t in range(NT)]
        Dxs = [k.dram(None, "dxs%d" % s) for s in range(NSB)]
        Dada = k.dram(None, "dada")
        Dq = [k.dram(None, "dq%d" % h) for h in range(H)]
        Dz = [k.dram(None, "dz%d" % t) for t in range(NT)]
        Do = [k.dram(None, "do%d" % t) for t in range(NT)]
        Dout = k.dram(None, "dout")
        Din = k.dram(None, "din")

        banks = []
        for i in range(8):
            t = stack.enter_context(nc.psum_tensor("ps%d" % i, [128, 512], F32))
            banks.append(Buf(t, "ps%d" % i, False, k))
            banks[-1].psum = True

        def ps(hold=False):
            while (k.nps % 8) in k.held:
                k.nps += 1
            i = k.nps % 8
            k.nps += 1
            if hold:
                k.held.add(i)
            return banks[i]

        def unhold(bs):
            for b in bs:
                k.held.discard(banks.index(b))

        def psf(b, p, n):
            return b.v(b.t[0:p, 0:n])

        def psb(b):
            return b.t[:].bitcast(BF16)

        cst_f = k.sb("cst_f", [128, 5, 128], F32, dma=True)
        k.dma(cst_f[:, :, :], Din.v(cst.rearrange("c p f -> p c f")))
        ident_f = cst_f.v(cst_f.t[:, 0, :])
        U_f = cst_f.v(cst_f.t[:, 1, :])
        ones_f = cst_f.v(cst_f.t[:, 4, :])
        ident_b = k.sb("ident_b", [128, 128], BF16)
        ones_b = k.sb("ones_b", [128, 128], BF16)
        k.cp("dve", ident_b[:, :], ident_f)
        k.cp("dve", ones_b[:, :], ones_f)

        def mneg_T(C):
            return cst_f.v(cst_f.t[0:C, 2, 0:C])

        def mpos_S(C):
            return cst_f.v(cst_f.t[0:C, 3, 0:C])

        W_in = k.sb("W_in", [128, KC, GIN], BF16)
        W_out = k.sb("W_out", [128, KC, D], BF16)
        stg = None
        wvec = k.sb("wvec", [128, D], F32, dma=True)
        shiftv = k.sb("shiftv", [128, D], F32, dma=True)
        gvec = k.sb("gvec", [128, D], F32, dma=True)
        xt0 = k.sb("xt0", [128, D], F32, dma=True)
        xt = [xt0, xt0]
        t1 = k.sb("t1", [128, D], F32, dma=True)
        hb = k.sb("hb", [128, D], BF16)
        hT = k.sb("hT", [128, KC, 128], BF16)
        og = k.sb("og", [128, D], BF16)
        ogT = k.sb("ogT", [128, KC, 128], BF16)
        col = k.sb("col", [128, 16], F32)
        stg_i = [0]

        def load_weight(dst, src_ap, ncols, stg_=None):
            CH = 1028
            for kc in range(KC):
                for c0 in range(0, ncols, CH):
                    cw = min(CH, ncols - c0)
                    s = (stg_ or stg)[stg_i[0] % 2]
                    stg_i[0] += 1
                    k.dma(s[:, 0:cw], Din.v(src_ap[:, kc, c0:c0 + cw]))
                    eng = "dve" if (stg_i[0] % 2 == 0) else "pool"
                    k.cp(eng, dst[:, kc, c0:c0 + cw], s[:, 0:cw])

        NS = 1 + NSB
        with contextlib.ExitStack() as st2:
            k2 = k
            old = k.stack
            k.stack = st2
            c3_sb = k.sb("c3_sb", [128, KC, NS], F32, dma=True)
            cact = k.sb("cact", [128, KC, NS], F32)
            wst = [k.sb("wst%d" % i, [128, 1536], F32, dma=True) for i in range(2)]
            bada = k.sb("bada", [NS, 3 * D], F32, dma=True)
            arow = k.sb("arow", [NS, 3 * D], F32, dma=True)
            k.dma(c3_sb[:, :, :], Din.v(c3))
            k.actf(cact[:, :, :], c3_sb[:, :, :], AF.Silu)
            wi = 0
            for l in range(depth):
                k.dma(bada[:, :], Din.v(b_ada[l:l + 1, :].to_broadcast([NS, 3 * D])))
                for half in range(2):
                    pbs = [ps() for _ in range(3)]
                    for kc in range(KC):
                        s = wst[wi % 2]
                        wi += 1
                        k.dma(s[:, :], Din.v(w_ada[l, :, kc, half * 1536:(half + 1) * 1536]))
                        for j in range(3):
                            k.mm(psf(pbs[j], NS, 512), cact[:, kc, :], s[:, j * 512:(j + 1) * 512],
                                 start=(kc == 0), stop=(kc == KC - 1))
                    for j in range(3):
                        c0 = half * 1536 + j * 512
                        k.tt("dve", arow[:, c0:c0 + 512], psf(pbs[j], NS, 512), bada[:, c0:c0 + 512], ALU.add)
                k.dma(Dada.v(ada_d[l, :, :]), arow[:, :])
            k.barrier()
            k.stack = old

        def load_mod(l, s, C):
            k.dma(shiftv[0:C, :], Dada.v(ada_d[l, s:s + 1, 0:D].to_broadcast([C, D])))
            k.dma(wvec[0:C, :], Dada.v(ada_d[l, s:s + 1, D:2 * D].to_broadcast([C, D])))
            k.dma(gvec[0:C, :], Dada.v(ada_d[l, s:s + 1, 2 * D:3 * D].to_broadcast([C, D])))
            k.dma(t1[0:C, :], Din.v(norm_pre[l:l + 1, :].to_broadcast([C, D])))
            k.stt("dve", wvec[0:C, :], wvec[0:C, :], 1.0, t1[0:C, :], ALU.add, ALU.mult)
            k.dma(t1[0:C, :], Din.v(norm_post[l:l + 1, :].to_broadcast([C, D])))
            k.tt("dve", gvec[0:C, :], gvec[0:C, :], t1[0:C, :], ALU.mult)

        def rstd_from_ss(ssv, C, n, dst):
            k.actf(dst, ssv, AF.Ln, bias=EPS, scale=1.0 / n)
            k.actf(dst, dst, AF.Exp, scale=-0.5)

        def norm_mod_T(x, C):
            k.actf(t1[0:C, :], x[0:C, :], AF.Square, accum=col[0:C, 0:1])
            rstd_from_ss(col[0:C, 0:1], C, D, col[0:C, 1:2])
            k.stt("dve", t1[0:C, :], x[0:C, :], col[0:C, 1:2], wvec[0:C, :], ALU.mult, ALU.mult)
            k.tt("pool", hb[0:C, :], t1[0:C, :], shiftv[0:C, :], ALU.add)
            to_T(hb, hT, C)

        def to_T(src, dst, C):
            pb = ps()
            pv = psb(pb)
            for kc in range(KC):
                k.tr(pb.v(pv[:, kc * 128:kc * 128 + C]), src[0:C, kc * 128:(kc + 1) * 128], ident_b[0:C, 0:C])
            k.cp("act", dst[:, :, 0:C], pb.v(pv.rearrange("p (k c) -> p k c", k=KC)[:, :, 0:C]))

        def out_proj_post(x, C, dst_dram_v):
            to_T(og, ogT, C)
            pbs = [ps(), ps()]
            for j in range(2):
                for kc in range(KC):
                    k.mm(psf(pbs[j], C, 512), ogT[:, kc, 0:C], W_out[:, kc, j * 512:(j + 1) * 512],
                         start=(kc == 0), stop=(kc == KC - 1))
            for j in range(2):
                k.actf(t1[0:C, j * 512:(j + 1) * 512], psf(pbs[j], C, 512), AF.Square, accum=col[0:C, 2 + j:3 + j])
            k.tt("dve", col[0:C, 4:5], col[0:C, 2:3], col[0:C, 3:4], ALU.add)
            rstd_from_ss(col[0:C, 4:5], C, D, col[0:C, 5:6])
            for j in range(2):
                k.stt("dve", t1[0:C, j * 512:(j + 1) * 512], psf(pbs[j], C, 512), col[0:C, 5:6],
                      gvec[0:C, j * 512:(j + 1) * 512], ALU.mult, ALU.mult)
            k.tt("pool", t1[0:C, :], t1[0:C, :], x[0:C, :], ALU.add)
            k.dma(dst_dram_v, t1[0:C, :])

        def gdn_layer(l, j, src_p, dst_p, src_s, dst_s):
            with contextlib.ExitStack() as st2:
                old = k.stack
                k.stack = st2
                HG = 4
                Dg = k.sb("Dg", [128, 24, 4, 128], BF16)
                msk = k.sb("msk", [128, 14, 128], BF16, dma=False)
                with contextlib.ExitStack() as st3:
                    k.stack = st3
                    stg_l = [k.sb("stgl%d" % i, [128, 1028], F32, dma=True) for i in range(2)]
                    mskf = k.sb("mskf", [128, 14, 128], F32, dma=True)
                    k.dma(mskf[:, :, :], Din.v(lvmask.rearrange("c p f -> p c f")))
                    k.cp("pool", msk[:, :, :], mskf[:, :, :])
                    load_weight(W_in, w_in_gdn[j], GIN, stg_l)
                    load_weight(W_out, w_out_gdn[j], D, stg_l)
                    k.barrier()
                    k.stack = st2
                cw = k.sb("cw", [128, 24, 4], F32, dma=True)
                uext = k.sb("uext", [128, 24, 131], BF16)
                uext_s = k.sb("uext_s", [128, 24, 3 + TS], BF16)
                cst32 = k.sb("cst32", [128, 24, 3], F32, dma=True)
                zs = k.sb("zs", [128, D], F32)
                gsm = k.sb("gsm", [128, 12, H], F32, dma=True)
                Gbc = k.sb("Gbc", [128, H, 128], F32)
                eGbc = k.sb("eGbc", [128, H, 128], BF16)
                onb = k.sb("onb", [128, 128], F32, dma=True)
                Sf = k.sb("Sf", [128, H, 128], F32, dma=True)
                Sb = k.sb("Sb", [128, H, 128], BF16)

                def mk(name, shape, dt):
                    return [k.sb("%s_%d" % (name, i), shape, dt) for i in range(HG)]
                qkv = mk("qkv", [128, 3, 128], F32)
                sq = mk("sq", [128, 128], BF16)
                rn = mk("rn", [128, 128], F32)
                qT = mk("qT", [128, 128], BF16)
                kT = mk("kT", [128, 128], BF16)
                vbf = mk("vbf", [128, 128], BF16)
                kd = mk("kd", [128, 128], BF16)
                kbg = mk("kbg", [128, 128], BF16)
                vb = mk("vb", [128, 128], BF16)
                arg = mk("arg", [128, 128], F32)
                decT = mk("decT", [128, 128], BF16)
                dec = mk("dec", [128, 128], BF16)
                Am = [mk("Am%d" % i, [128, 128], BF16) for i in range(2)]
                Bm = [mk("Bm%d" % i, [128, 128], BF16) for i in range(2)]
                Pm = [mk("Pm%d" % i, [128, 128], BF16) for i in range(2)]
                Dm = [mk("Dm%d" % i, [128, 128], BF16) for i in range(2)]
                Xm = [mk("Xm%d" % i, [128, 128], BF16) for i in range(2)]
                ui = mk("ui", [128, 128], F32)
                wT = mk("wT", [128, 128], BF16)
                qgT = mk("qgT", [128, 128], BF16)
                qkT = mk("qkT", [128, 128], BF16)
                ub = mk("ub", [128, 128], BF16)
                onf = mk("onf", [128, 128], F32)

                k.dma(cw[:, :, :], Din.v(conv_w[j]))
                for c in range(24):
                    for tap in range(4):
                        k.ts("dve" if (c + tap) % 2 else "pool", Dg[:, c, tap, :], ident_f, cw[:, c, tap:tap + 1], ALU.mult)
                k.dma(gsm[:, 0, :], Din.v(dt_bias[j:j + 1, :].to_broadcast([128, H])))
                k.dma(gsm[:, 1, :], Din.v(a_log[j:j + 1, :].to_broadcast([128, H])))
                k.actf(gsm[:, 1, :], gsm[:, 1, :], AF.Exp)
                k.ts("dve", gsm[:, 1, :], gsm[:, 1, :], -1.0, ALU.mult)
                k.dma(onb[:, :], Din.v(onorm[j:j + 1, :].to_broadcast([128, 128])))

                def gUv(hh, C):
                    return t1.v(t1.t[0:C, :].rearrange("p (h c) -> p h c", h=H)[:, hh, 0:C])

                def gdn_tile(x, C, ue):
                    norm_mod_T(x, C)
                    if DBG_CUT <= 1:
                        return
                    for g in range(6):
                        pb = ps()
                        for c4 in range(4):
                            ch = 4 * g + c4
                            for kc in range(KC):
                                k.mm(pb.v(pb.t[:, c4 * 128:c4 * 128 + C]), W_in[:, kc, ch * 128:(ch + 1) * 128],
                                     hT[:, kc, 0:C], start=(kc == 0), stop=(kc == KC - 1))
                        k.cp("act", ue[:, 4 * g:4 * g + 4, 3:3 + C],
                             pb.v(pb.t[:].rearrange("p (a c) -> p a c", a=4)[:, :, 0:C]))
                    if DBG_CUT <= 2:
                        return
                    for jb in range(2):
                        pb = ps()
                        for kc in range(KC):
                            k.mm(psf(pb, C, 512), hT[:, kc, 0:C], W_in[:, kc, 3072 + jb * 512:3072 + (jb + 1) * 512],
                                 start=(kc == 0), stop=(kc == KC - 1))
                        k.actf(zs[0:C, jb * 512:(jb + 1) * 512], psf(pb, C, 512), AF.Silu)
                    pab = ps()
                    for kc in range(KC):
                        k.mm(psf(pab, C, 16), hT[:, kc, 0:C], W_in[:, kc, 4096:4112], start=(kc == 0), stop=(kc == KC - 1))
                    G = lambda i: gsm[0:C, i, :]
                    k.tt("dve", G(2), pab.v(pab.t[0:C, 0:H]), G(0), ALU.add)
                    k.actf(G(2), G(2), AF.Exp)
                    k.actf(G(2), G(2), AF.Ln, bias=1.0)
                    k.tt("dve", G(3), G(2), G(1), ALU.mult)
                    k.actf(G(4), pab.v(pab.t[0:C, H:2 * H]), AF.Exp, scale=-1.0)
                    k.ts("dve", G(4), G(4), 1.0, ALU.add)
                    k.recip(G(4), G(4))
                    pg = ps()
                    k.mm(psf(pg, C, H), cst_f.v(cst_f.t[0:C, 1, 0:C]), G(3))
                    k.cp("dve", G(5), psf(pg, C, H))
                    k.mm(pg.v(pg.t[:, 16:16 + H]), cst_f.v(cst_f.t[0:C, 4, :]), G(3))
                    k.cp("dve", gsm[:, 6, :], pg.v(pg.t[:, 16:16 + H]))
                    k.actf(gsm[:, 7, :], gsm[:, 6, :], AF.Exp)
                    k.actf(G(8), G(5), AF.Exp)
                    k.tt("dve", G(8), G(8), G(4), ALU.mult)
                    k.tt("dve", G(9), gsm[0:C, 6, :], G(5), ALU.subtract)
                    k.actf(G(9), G(9), AF.Exp)
                    if DBG_CUT <= 3:
                        return
                    for hh in range(H):
                        k.ts("dve" if hh % 2 else "pool", gUv(hh, C), cst_f.v(cst_f.t[0:C, 1, 0:C]),
                             gsm[0:C, 3, hh:hh + 1], ALU.mult)
                    hpm = max(1, 512 // C)
                    for h0 in range(0, H, hpm):
                        hn = min(hpm, H - h0)
                        pb = ps()
                        for hh in range(hn):
                            k.mm(pb.v(pb.t[:, hh * C:(hh + 1) * C]), cst_f.v(cst_f.t[0:C, 4, :]), gUv(h0 + hh, C))
                        k.cp("act", Gbc[:, h0:h0 + hn, 0:C], pb.v(pb.t[:, 0:hn * C].rearrange("p (h c) -> p h c", h=hn)))
                        k.actf(eGbc[:, h0:h0 + hn, 0:C], pb.v(pb.t[:, 0:hn * C].rearrange("p (h c) -> p h c", h=hn)), AF.Exp)
                    if DBG_CUT <= 4:
                        return
                    nlev = 0
                    while (1 << nlev) < C:
                        nlev += 1

                    def s_conv(h, s, st):
                        pc = ps()
                        for i3, ch in enumerate((h, 8 + h, 16 + h)):
                            for tap in range(4):
                                k.mm(pc.v(pc.t[:, i3 * 128:i3 * 128 + C]), Dg[:, ch, tap, :], ue[:, ch, tap:tap + C],
                                     start=(tap == 0), stop=(tap == 3))
                        yield
                        k.actf(qkv[s][:, :, 0:C], pc.v(pc.t[:, 0:384].rearrange("p (a c) -> p a c", a=3)[:, :, 0:C]), AF.Silu)

                    def s_norm(i2):
                        def f(h, s, st):
                            dstT, sc = ((qT, 128 ** -0.5), (kT, 1.0))[i2]
                            k.actf(sq[s][:, 0:C], qkv[s][:, i2, 0:C], AF.Square)
                            if i2 == 1:
                                k.cp("pool", vbf[s][:, 0:C], qkv[s][:, 2, 0:C])
                            yield
                            pn = ps()
                            k.mm(pn.v(pn.t[:, 0:C]), ones_b[:, :], sq[s][:, 0:C])
                            yield
                            k.actf(rn[s][:, 0:C], pn.v(pn.t[:, 0:C]), AF.Ln, bias=1e-6)
                            yield
                            k.actf(rn[s][:, 0:C], rn[s][:, 0:C], AF.Exp, scale=-0.5)
                            yield
                            k.stt("dve", dstT[s][:, 0:C], qkv[s][:, i2, 0:C], sc, rn[s][:, 0:C], ALU.mult, ALU.mult)
                        return f

                    def s_tok(h, s, st):
                        pt = ps()
                        ptv = psb(pt)
                        k.tr(pt.v(ptv[0:C, 0:128]), kT[s][:, 0:C], ident_b[:, :])
                        k.tr(pt.v(ptv[0:C, 128:256]), vbf[s][:, 0:C], ident_b[:, :])
                        k.stt("dve", arg[s][0:C, 0:C], Gbc[0:C, h, 0:C], gsm[0:C, 5, h:h + 1], mneg_T(C), ALU.subtract, ALU.add)
                        k.tt("pool", qgT[s][:, 0:C], qT[s][:, 0:C], eGbc[:, h, 0:C], ALU.mult)
                        yield
                        k.actf(decT[s][0:C, 0:C], arg[s][0:C, 0:C], AF.Exp)
                        k.ts("dve", kd[s][0:C, :], pt.v(ptv[0:C, 0:128]), gsm[0:C, 9, h:h + 1], ALU.mult)
                        k.ts("dve", kbg[s][0:C, :], pt.v(ptv[0:C, 0:128]), gsm[0:C, 8, h:h + 1], ALU.mult)
                        k.ts("dve", vb[s][0:C, :], pt.v(ptv[0:C, 128:256]), gsm[0:C, 4, h:h + 1], ALU.mult)
                        yield
                        k.stt("dve", arg[s][0:C, 0:C], Gbc[0:C, h, 0:C], gsm[0:C, 5, h:h + 1], mpos_S(C), ALU.subtract, ALU.add)
                        yield
                        k.actf(dec[s][0:C, 0:C], arg[s][0:C, 0:C], AF.Exp, scale=-1.0)

                    def s_kk(h, s, st):
                        pk = ps()
                        k.mm(pk.v(pk.t[0:C, 0:C]), kT[s][:, 0:C], kT[s][:, 0:C])
                        k.mm(pk.v(pk.t[0:C, 128:128 + C]), kT[s][:, 0:C], qT[s][:, 0:C])
                        yield
                        k.stt("dve", Am[0][s][0:C, 0:C], pk.v(pk.t[0:C, 0:C]), gsm[0:C, 4, h:h + 1], dec[s][0:C, 0:C],
                              ALU.mult, ALU.mult)
                        k.tt("dve", qkT[s][0:C, 0:C], pk.v(pk.t[0:C, 128:128 + C]), decT[s][0:C, 0:C], ALU.mult)
                        yield
                        pB = ps()
                        pBv = psb(pB)
                        k.tr(pB.v(pBv[0:C, 0:C]), Am[0][s][0:C, 0:C], ident_b[0:C, 0:C])
                        yield
                        k.cp("act", Bm[0][s][0:C, 0:C], pB.v(pBv[0:C, 0:C]))
                        st["cd"] = 0

                    def s_level(lv):
                        def f(h, s, st):
                            cd = st["cd"]
                            k.tt("pool", Am[1][s][0:C, 0:C], Am[0][s][0:C, 0:C], msk[0:C, 2 * lv, 0:C], ALU.mult)
                            k.tt("dve", Bm[1][s][0:C, 0:C], Bm[0][s][0:C, 0:C], msk[0:C, 2 * lv + 1, 0:C], ALU.mult)
                            yield
                            if lv == 0:
                                k.tt("pool", Dm[0][s][0:C, 0:C], ident_b[0:C, 0:C], Am[1][s][0:C, 0:C], ALU.subtract)
                                k.tt("dve", Pm[0][s][0:C, 0:C], ident_b[0:C, 0:C], Bm[1][s][0:C, 0:C], ALU.subtract)
                                st["cd"] = 0
                                return
                            px = ps()
                            k.mm(px.v(px.t[0:C, 0:C]), Am[1][s][0:C, 0:C], Pm[cd][s][0:C, 0:C])
                            k.mm(px.v(px.t[0:C, 128:128 + C]), Bm[1][s][0:C, 0:C], Dm[cd][s][0:C, 0:C])
                            yield
                            k.cp("act", Xm[0][s][0:C, 0:C], px.v(px.t[0:C, 0:C]))
                            k.cp("act", Xm[1][s][0:C, 0:C], px.v(px.t[0:C, 128:128 + C]))
                            yield
                            py = ps()
                            k.mm(py.v(py.t[0:C, 0:C]), Dm[cd][s][0:C, 0:C], Xm[0][s][0:C, 0:C])
                            k.mm(py.v(py.t[0:C, 128:128 + C]), Pm[cd][s][0:C, 0:C], Xm[1][s][0:C, 0:C])
                            yield
                            k.stt("dve", Pm[1 - cd][s][0:C, 0:C], py.v(py.t[0:C, 0:C]), -1.0, Pm[cd][s][0:C, 0:C], ALU.mult, ALU.add)
                            k.stt("dve", Dm[1 - cd][s][0:C, 0:C], py.v(py.t[0:C, 128:128 + C]), -1.0, Dm[cd][s][0:C, 0:C],
                                  ALU.mult, ALU.add)
                            st["cd"] = 1 - cd
                        return f

                    def s_uw(h, s, st):
                        TTm = Pm[st["cd"]][s]
                        pu = ps()
                        k.mm(pu.v(pu.t[0:C, 0:128]), TTm[0:C, 0:C], vb[s][0:C, :])
                        k.mm(pu.v(pu.t[:, 128:128 + C]), kbg[s][0:C, :], TTm[0:C, 0:C])
                        yield
                        k.cp("act", ui[s][0:C, :], pu.v(pu.t[0:C, 0:128]))
                        k.cp("act", wT[s][:, 0:C], pu.v(pu.t[:, 128:128 + C]))

                    def s_scan(h, s, st):
                        pw = ps()
                        k.mm(pw.v(pw.t[0:C, 0:128]), wT[s][:, 0:C], Sb[:, h, :])
                        k.mm(pw.v(pw.t[0:C, 128:256]), qgT[s][:, 0:C], Sb[:, h, :], start=True, stop=False)
                        k.ts("dve", Sf[:, h, :], Sf[:, h, :], gsm[:, 7, h:h + 1], ALU.mult)
                        yield
                        k.stt("dve", ub[s][0:C, :], pw.v(pw.t[0:C, 0:128]), -1.0, ui[s][0:C, :], ALU.mult, ALU.add)
                        yield
                        k.mm(pw.v(pw.t[0:C, 128:256]), qkT[s][0:C, 0:C], ub[s][0:C, :], start=False, stop=True)
                        k.mm(pw.v(pw.t[:, 256:384]), kd[s][0:C, :], ub[s][0:C, :])
                        yield
                        k.tt("dve", Sf[:, h, :], pw.v(pw.t[:, 256:384]), Sf[:, h, :], ALU.add)
                        c0 = 8 + 2 * s
                        k.actf(onf[s][0:C, :], pw.v(pw.t[0:C, 128:256]), AF.Square, accum=col[0:C, c0:c0 + 1])
                        yield
                        k.cp("pool", Sb[:, h, :], Sf[:, h, :])
                        k.actf(col[0:C, c0 + 1:c0 + 2], col[0:C, c0:c0 + 1], AF.Ln, bias=EPS, scale=1.0 / 128)
                        yield
                        k.actf(col[0:C, c0 + 1:c0 + 2], col[0:C, c0 + 1:c0 + 2], AF.Exp, scale=-0.5)
                        yield
                        k.stt("dve", onf[s][0:C, :], pw.v(pw.t[0:C, 128:256]), col[0:C, c0 + 1:c0 + 2], onb[0:C, :], ALU.mult, ALU.mult)
                        yield
                        k.tt("pool", og[0:C, h * 128:(h + 1) * 128], onf[s][0:C, :], zs[0:C, h * 128:(h + 1) * 128], ALU.mult)

                    stages = [s_conv, s_norm(0), s_norm(1), s_tok, s_kk]
                    stages += [s_level(lv) for lv in range(nlev)]
                    stages += [s_uw, s_scan]
                    for g0 in range(0, H, HG):
                        hs = list(range(g0, min(H, g0 + HG)))
                        sts = {h: {} for h in hs}
                        for stg_f in stages:
                            alive = [stg_f(h, h - g0, sts[h]) for h in hs]
                            while alive:
                                nxt = []
                                for gen in alive:
                                    try:
                                        next(gen)
                                        nxt.append(gen)
                                    except StopIteration:
                                        pass
                                alive = nxt
                    k.cp("pool", cst32[:, :, :], ue[:, :, C:C + 3])
                    k.cp("pool", ue[:, :, 0:3], cst32[:, :, :])

                if DBG_STAGE < 3:
                    k.barrier()
                    k.stack = old
                    return
                load_mod(l, 0, 128)
                k.memset("pool", Sf[:, :, :], 0.0)
                k.memset("pool", Sb[:, :, :], 0.0)
                k.memset("pool", uext[:, :, 0:3], 0.0)
                for t in range(NT if DBG_STAGE >= 4 else 0):
                    x = xt[t % 2]
                    k.dma(x[:, :], Dx[t].v(src_p[t * 128:(t + 1) * 128, :]))
                    gdn_tile(x, 128, uext)
                    out_proj_post(x, 128, Dx[t].v(dst_p[t * 128:(t + 1) * 128, :]))
                k.dma(Dout.v(st_p[j].rearrange("h k v -> k h v")), Sf[:, :, :])
                for jj in range(3):
                    k.dma(Dout.v(conv_p[j, jj].rearrange("(c p) -> p c", p=128)), cst32[:, :, jj], slow=True)
                for s in range(0 if DBG_NOSAMPLE else NSB):
                    load_mod(l, 1 + s, TS)
                    k.dma(Sf[:, :, :], Din.v(st_in[j, s].rearrange("h k v -> k h v")))
                    k.cp("pool", Sb[:, :, :], Sf[:, :, :])
                    k.dma(cst32[:, :, :], Din.v(cc_in[j, s]))
                    k.cp("pool", uext_s[:, :, 0:3], cst32[:, :, :])
                    x = xt[s % 2]
                    k.dma(x[0:TS, :], Dxs[s].v(src_s[s]))
                    gdn_tile(x, TS, uext_s)
                    out_proj_post(x, TS, Dxs[s].v(dst_s[s]))
                    k.dma(Dout.v(st_s[j, s].rearrange("h k v -> k h v")), Sf[:, :, :])
                    for jj in range(3):
                        k.dma(Dout.v(conv_s[j, s, jj].rearrange("(c p) -> p c", p=128)), cst32[:, :, jj], slow=True)
                k.barrier()
                k.stack = old

        def diff_layer(l, j, src_p, dst_p, src_s, dst_s):
            lam_init = 0.8 - 0.6 * math.exp(-0.3 * l)
            with contextlib.ExitStack() as st2:
                old = k.stack
                k.stack = st2
                QS = min(256, T)
                NQ = QS // 128
                lq = k.sb("lq", [128, 4, 64], F32, dma=True)
                lamc = k.sb("lamc", [128, 4], F32)
                snb = k.sb("snb", [128, 128], F32, dma=True)
                cs = k.sb("cs", [128, 2, 32], F32, dma=True)
                pr = k.sb("pr", [128, 2, D], F32, dma=True)
                vz = k.sb("vz", [128, 2, D], F32, dma=True)
                qkb = k.sb("qkb", [128, 2, D], BF16)
                vb16 = k.sb("vb16", [128, D], BF16, dma=True)
                qkT = k.sb("qkT_t", [128, 2, H, 128], BF16, dma=True)
                KTh = k.sb("KTh", [128, max(T, PAST + TS)], BF16, dma=True)
                Vh = k.sb("Vh", [128, max(NT, NPT + 1), 132], BF16, dma=True)
                QTh = k.sb("QTh", [128, QS], BF16, dma=True)
                PT = [k.sb("PT%d" % i, [128, QS], BF16) for i in range(4)]
                osb = [k.sb("osb%d" % i, [128, 2, 132], F32) for i in range(4)]
                odf = [k.sb("odf%d" % i, [128, 128], F32, dma=True) for i in range(2)]
                o_t = k.sb("o_t", [128, D], F32, dma=True)
                onf = k.sb("onf2", [128, 128], F32)

                with contextlib.ExitStack() as st3:
                    k.stack = st3
                    stg_l = [k.sb("stgl%d" % i, [128, 1028], F32, dma=True) for i in range(2)]
                    load_weight(W_in, w_in_diff[j], DIN, stg_l)
                    load_weight(W_out, w_out_diff[j], D, stg_l)
                    k.barrier()
                    k.stack = st2
                k.dma(lq[:, :, :], Din.v(lamv[j:j + 1].to_broadcast([128, 4, 64])))
                k.tt("dve", lq[:, 0, :], lq[:, 0, :], lq[:, 1, :], ALU.mult)
                k.tt("dve", lq[:, 2, :], lq[:, 2, :], lq[:, 3, :], ALU.mult)
                k.rsum(lamc[:, 0:1], lq[:, 0, :])
                k.rsum(lamc[:, 1:2], lq[:, 2, :])
                k.actf(lamc[:, 0:2], lamc[:, 0:2], AF.Exp)
                k.tt("dve", lamc[:, 2:3], lamc[:, 0:1], lamc[:, 1:2], ALU.subtract)
                k.ts("dve", lamc[:, 2:3], lamc[:, 2:3], -1.0, ALU.mult, -lam_init, ALU.add)
                k.dma(snb[:, :], Din.v(subln[j:j + 1, :].to_broadcast([128, 128])))
                k.ts("dve", snb[:, :], snb[:, :], 1.0 - lam_init, ALU.mult)

                def proj_tile(x, C, rope_ap, kdst, vdst):
                    norm_mod_T(x, C)
                    k.dma(cs[0:C, :, :], Din.v(rope_ap.rearrange("a t d -> t a d")))
                    for blk in range(8):
                        pb = ps()
                        for kc in range(KC):
                            k.mm(psf(pb, C, 512), hT[:, kc, 0:C], W_in[:, kc, blk * 512:(blk + 1) * 512],
                                 start=(kc == 0), stop=(kc == KC - 1))
                        if blk < 4:
                            k.cp("act", pr[0:C, blk // 2, (blk % 2) * 512:(blk % 2 + 1) * 512], psf(pb, C, 512))
                        elif blk < 6:
                            k.cp("act", vz[0:C, 0, (blk - 4) * 512:(blk - 3) * 512], psf(pb, C, 512))
                        else:
                            k.actf(vz[0:C, 1, (blk - 6) * 512:(blk - 5) * 512], psf(pb, C, 512), AF.Silu)
                    cosb = cs.v(cs.t[0:C, 0, :].unsqueeze(1).to_broadcast([C, 16, 32]))
                    sinb = cs.v(cs.t[0:C, 1, :].unsqueeze(1).to_broadcast([C, 16, 32]))
                    rtA = lambda i: o_t.v(o_t.t[0:C, :].rearrange("p (a g d) -> p a g d", a=2, g=16)[:, i])
                    rtB = lambda i: t1.v(t1.t[0:C, :].rearrange("p (a g d) -> p a g d", a=2, g=16)[:, i])
                    for i2 in range(2):
                        xv = pr.t[0:C, i2, :].rearrange("p (g two d) -> p g two d", g=16, two=2)
                        x1 = pr.v(xv[:, :, 0, :])
                        x2 = pr.v(xv[:, :, 1, :])
                        e1, e2 = ("dve", "pool") if i2 == 0 else ("pool", "dve")
                        k.tt(e1, rtA(0), x1, cosb, ALU.mult)
                        k.tt(e1, rtA(1), x2, sinb, ALU.mult)
                        k.tt(e2, rtB(0), x2, cosb, ALU.mult)
                        k.tt(e2, rtB(1), x1, sinb, ALU.mult)
                        k.tt(e1, x1, rtA(0), rtA(1), ALU.subtract)
                        k.tt(e2, x2, rtB(0), rtB(1), ALU.add)
                    k.dma(kdst, pr[0:C, 1, :])
                    k.dma(vdst, vz[0:C, 0, :])
                    k.ts("dve", qkb[0:C, 0, :], pr[0:C, 0, :], 0.125, ALU.mult)
                    k.cp("pool", qkb[0:C, 1, :], pr[0:C, 1, :])
                    k.cp("pool", vb16[0:C, :], vz[0:C, 0, :])
                    for i2 in range(2):
                        pb = ps()
                        pv = psb(pb)
                        for h in range(H):
                            k.tr(pb.v(pv[:, h * 128:h * 128 + C]), qkb[0:C, i2, h * 128:(h + 1) * 128], ident_b[0:C, 0:C])
                        k.cp("act", qkT[:, i2, :, 0:C], pb.v(pv.rearrange("p (h c) -> p h c", h=H)[:, :, 0:C]))

                def sub_out(C, ocomb, h):
                    k.actf(onf[0:C, :], ocomb, AF.Square, accum=col[0:C, 8:9])
                    rstd_from_ss(col[0:C, 8:9], C, 128, col[0:C, 9:10])
                    k.stt("dve", onf[0:C, :], ocomb, col[0:C, 9:10], snb[0:C, :], ALU.mult, ALU.mult)

                def combine(C, ob, dstv):
                    k.recip(ob[0:C, :, 129:130], ob[0:C, :, 128:129])
                    k.ts("dve", ob[0:C, 1, 129:130], ob[0:C, 1, 129:130], lamc[0:C, 2:3], ALU.mult)
                    k.ts("dve", ob[0:C, 0, 0:128], ob[0:C, 0, 0:128], ob[0:C, 0, 129:130], ALU.mult)
                    k.stt("dve", dstv, ob[0:C, 1, 0:128], ob[0:C, 1, 129:130], ob[0:C, 0, 0:128], ALU.mult, ALU.add)

                if DBG_D <= 1:
                    k.barrier()
                    k.stack = old
                    return
                load_mod(l, 0, 128)
                for t in range(NT):
                    x = xt[t % 2]
                    k.dma(x[:, :], Dx[t].v(src_p[t * 128:(t + 1) * 128, :]))
                    proj_tile(x, 128, rope_p[:, t * 128:(t + 1) * 128, :],
                              Dout.v(k_p[j, t * 128:(t + 1) * 128, :]), Dout.v(v_p[j, t * 128:(t + 1) * 128, :]))
                    k.dma(Dq[0].v(qT_d[:, :, t * 128:(t + 1) * 128].rearrange("h p c -> p h c")), qkT[:, 0, :, :])
                    k.dma(Dq[0].v(kT_d[:, :, t * 128:(t + 1) * 128].rearrange("h p c -> p h c")), qkT[:, 1, :, :])
                    k.dma(Dq[0].v(vb_d[:, t * 128:(t + 1) * 128, :].rearrange("h p d -> p h d")),
                          vb16.v(vb16.t[:, :].rearrange("p (h d) -> p h d", h=H)))
                    k.dma(Dz[t].v(z_d[t * 128:(t + 1) * 128, :]), vz[:, 1, :])
                if DBG_D <= 2:
                    k.barrier()
                    k.stack = old
                    return
                k.memset("pool", Vh[:, :, 128:129], 1.0)
                for h in range(H):
                    k.dma(KTh[:, 0:T], Dq[0].v(kT_d[h]))
                    k.dma(Vh[:, 0:NT, 0:128], Dq[0].v(vb_d[h].rearrange("(t p) d -> p t d", p=128)))
                    for qs in range(T // QS):
                        k.dma(QTh[:, :], Dq[0].v(qT_d[h, :, qs * QS:(qs + 1) * QS]))
                        nkv = (qs + 1) * NQ
                        pso = [[ps(hold=True) for _ in range(NQ)] for _c in range(2)]

                        def emit_qk(jt_):
                            r0_ = max(0, jt_ - qs * NQ)
                            pp_ = []
                            for c_ in range(2):
                                p_ = ps()
                                k.mm(p_.v(p_.t[:, r0_ * 128:QS]), KTh[c_ * 64:(c_ + 1) * 64, jt_ * 128:(jt_ + 1) * 128],
                                     QTh[c_ * 64:(c_ + 1) * 64, r0_ * 128:QS])
                                pp_.append(p_)
                            return pp_
                        pend = emit_qk(0)
                        for jt in range(nkv):
                            r0 = max(0, jt - qs * NQ)
                            pst2 = pend
                            if jt + 1 < nkv:
                                pend = emit_qk(jt + 1)
                            for c in range(2):
                                pst = pst2[c]
                                pt_ = PT[(2 * jt + c) % 4]
                                k.actf(pt_[:, r0 * 128:QS], pst.v(pst.t[:, r0 * 128:QS]), AF.Exp)
                                if jt >= qs * NQ:
                                    k.memset("pool", pt_[64:128, r0 * 128:r0 * 128 + 64], 0.0)
                                for r in range(r0, NQ):
                                    k.mm(pso[c][r].v(pso[c][r].t[:, 0:129]), pt_[:, r * 128:(r + 1) * 128],
                                         Vh[:, jt, 0:129], start=(jt == 0), stop=(jt == qs * NQ + r))
                        for c in range(2):
                            for r in range(NQ):
                                k.cp("act", osb[r][:, c, 0:129], pso[c][r].v(pso[c][r].t[:, 0:129]))
                            unhold(pso[c])
                        for r in range(NQ):
                            tq = qs * NQ + r
                            sl = (h * (T // 128) + tq) % 2
                            combine(128, osb[r], odf[sl][:, :])
                            k.dma(Do[tq].v(o_d[tq * 128:(tq + 1) * 128, h * 128:(h + 1) * 128]), odf[sl][:, :])
                if DBG_D <= 3:
                    k.barrier()
                    k.stack = old
                    return
                for t in range(NT):
                    x = xt[t % 2]
                    k.dma(x[:, :], Dx[t].v(src_p[t * 128:(t + 1) * 128, :]))
                    k.dma(o_t[:, :], Do[t].v(o_d[t * 128:(t + 1) * 128, :]))
                    k.dma(vz[:, 1, :], Dz[t].v(z_d[t * 128:(t + 1) * 128, :]))
                    for h in range(H):
                        sub_out(128, o_t[:, h * 128:(h + 1) * 128], h)
                        k.tt("pool", og[:, h * 128:(h + 1) * 128], onf[:, :], vz[:, 1, h * 128:(h + 1) * 128], ALU.mult)
                    out_proj_post(x, 128, Dx[t].v(dst_p[t * 128:(t + 1) * 128, :]))
                if DBG_D <= 4:
                    k.barrier()
                    k.stack = old
                    return
                for s in range(0 if DBG_NOSAMPLE else NSB):
                    C = TS
                    load_mod(l, 1 + s, C)
                    x = xt[s % 2]
                    k.dma(x[0:C, :], Dxs[s].v(src_s[s]))
                    proj_tile(x, C, rope_s, Dout.v(k_s[j, s]), Dout.v(v_s[j, s]))
                    k.memset("pool", Vh[:, :, 128:129], 1.0)
                    for h in range(H):
                        for c0 in range(0, PAST, 512):
                            k.dma(pr[:, 0, 0:512], Din.v(ck_in[j, s, h, :, c0:c0 + 512]))
                            k.cp("dve", KTh[:, c0:c0 + 512], pr[:, 0, 0:512])
                            k.dma(pr.v(pr.t[:, 1, 0:512].rearrange("p (t d) -> p t d", t=4)),
                                  Din.v(cv_in[j, s, c0:c0 + 512, h, :].rearrange("(t p) d -> p t d", p=128)))
                            k.cp("pool", Vh[:, c0 // 128:c0 // 128 + 4, 0:128],
                                 pr.v(pr.t[:, 1, 0:512].rearrange("p (t d) -> p t d", t=4)))
                        k.cp("dve", KTh[:, PAST:PAST + C], qkT[:, 1, h, 0:C])
                        k.cp("pool", Vh[0:C, NPT, 0:128], vb16[0:C, h * 128:(h + 1) * 128])
                        for c in range(2):
                            pso = ps(hold=True)
                            def emit_qk_s(jt_):
                                rows_ = 128 if jt_ < NPT else C
                                p_ = ps()
                                k.mm(p_.v(p_.t[0:rows_, 0:C]), KTh[c * 64:(c + 1) * 64, jt_ * 128:jt_ * 128 + rows_],
                                     qkT[c * 64:(c + 1) * 64, 0, h, 0:C])
                                return p_
                            pend = emit_qk_s(0)
                            for jt in range(NPT + 1):
                                rows = 128 if jt < NPT else C
                                pst = pend
                                if jt + 1 < NPT + 1:
                                    pend = emit_qk_s(jt + 1)
                                pt_ = PT[jt % 4]
                                k.actf(pt_[0:rows, 0:C], pst.v(pst.t[0:rows, 0:C]), AF.Exp)
                                k.mm(pso.v(pso.t[0:C, 0:129]), pt_[0:rows, 0:C], Vh[0:rows, jt, 0:129],
                                     start=(jt == 0), stop=(jt == NPT))
                            k.cp("act", osb[h % 4][0:C, c, 0:129], pso.v(pso.t[0:C, 0:129]))
                            unhold([pso])
                        combine(C, osb[h % 4], o_t[0:C, h * 128:(h + 1) * 128])
                    for h in range(H):
                        sub_out(C, o_t[0:C, h * 128:(h + 1) * 128], h)
                        k.tt("pool", og[0:C, h * 128:(h + 1) * 128], onf[0:C, :], vz[0:C, 1, h * 128:(h + 1) * 128], ALU.mult)
                    out_proj_post(x, C, Dxs[s].v(dst_s[s]))
                k.barrier()
                k.stack = old

        for l in range(depth if DBG_STAGE >= 2 else 0):
            src_p = xp if l == 0 else xa
            dst_p = y_p if l == depth - 1 else xa
            src_s = xs if l == 0 else xsa
            dst_s = y_s if l == depth - 1 else xsa
            if l % 2 == 0:
                gdn_layer(l, l // 2, src_p, dst_p, src_s, dst_s)
            else:
                diff_layer(l, l // 2, src_p, dst_p, src_s, dst_s)
        k.barrier()
        DBG['k'] = k
        print("instructions emitted:", k.nins)
    return nc


def _consts():
    i = np.arange(128)
    ident = np.eye(128, dtype=np.float32)
    U = (i[:, None] <= i[None, :]).astype(np.float32)
    mnegT = np.where(i[None, :] >= i[:, None], 0.0, NEG).astype(np.float32)
    mposS = np.where(i[None, :] < i[:, None], 0.0, -NEG).astype(np.float32)
    ones = np.ones((128, 128), np.float32)
    return np.stack([ident, U, mnegT, mposS, ones]).astype(np.float32)


def _lvmasks():
    i = np.arange(128)
    out = []
    sz = 1
    while sz < 128:
        m = ((i[:, None] // (2 * sz)) == (i[None, :] // (2 * sz))) & ((i[:, None] % (2 * sz)) >= sz) & ((i[None, :] % (2 * sz)) < sz)
        out.append(m.astype(np.float32))
        out.append(m.T.astype(np.float32))
        sz *= 2
    return np.stack(out).astype(np.float32)


def _rope_tab(pos):
    half = 32
    inv = (1.0 / (np.float32(10000.0) ** (np.arange(half, dtype=np.float32) / np.float32(half)))).astype(np.float32)
    ang = pos.astype(np.float32)[:, None] * inv[None, :]
    return np.stack([np.cos(ang), np.sin(ang)]).astype(np.float32)


def _wl(w):
    L, _, N = w.shape
    return np.ascontiguousarray(w.reshape(L, KC, 128, N).transpose(0, 2, 1, 3))


_PROG_CACHE = {}


def run(inputs, n_cores, T, PAST, NSB, TS, prompt_of_core, depth=DEPTH):
    key = (T, PAST, NSB, TS, depth)
    if key not in _PROG_CACHE:
        _PROG_CACHE[key] = build_program(T, PAST, NSB, TS, depth)
    nc = _PROG_CACHE[key]
    f = lambda a: np.ascontiguousarray(np.asarray(a, dtype=np.float32))
    shared = {
        "norm_pre": f(inputs["norm_pre"]), "norm_post": f(inputs["norm_post"]),
        "w_ada": _wl(f(inputs["w_ada"])), "b_ada": f(inputs["b_ada"]),
        "w_in_gdn": _wl(f(inputs["w_in_gdn"])),
        "conv_w": np.ascontiguousarray(f(inputs["conv_gdn"]).reshape(2, 4, 24, 128).transpose(0, 3, 2, 1)),
        "a_log": f(inputs["a_log_gdn"]), "dt_bias": f(inputs["dt_bias_gdn"]), "onorm": f(inputs["onorm_gdn"]),
        "w_out_gdn": _wl(f(inputs["w_out_gdn"])), "w_in_diff": _wl(f(inputs["w_in_diff"])),
        "lamv": np.ascontiguousarray(np.stack([f(inputs["lam_q1"]), f(inputs["lam_k1"]), f(inputs["lam_q2"]),
                                               f(inputs["lam_k2"])], axis=1)),
        "subln": f(inputs["subln_diff"]), "w_out_diff": _wl(f(inputs["w_out_diff"])),
        "cst": _consts(), "lvmask": _lvmasks(), "rope_p": _rope_tab(np.arange(T)), "rope_s": _rope_tab(PAST + np.arange(TS)),
    }
    xp = f(inputs["x_prompt"]); xs = f(inputs["x_sample"])
    cp = f(inputs["c_prompt"]); csm = f(inputs["c_sample"])
    stg = f(inputs["state_gdn"]); cc = f(inputs["cache_conv"]); ck = f(inputs["cache_k"]); cv = f(inputs["cache_v"])
    in_maps = []
    for c in range(n_cores):
        pb = prompt_of_core[c]
        sb = list(range(c * NSB, (c + 1) * NSB))
        cvec = np.stack([cp[pb]] + [csm[s] for s in sb], axis=-1)
        m = dict(shared)
        m["xp"] = xp[pb]
        m["xs"] = np.ascontiguousarray(xs[sb])
        m["c3"] = np.ascontiguousarray(cvec.reshape(KC, 128, 1 + NSB).transpose(1, 0, 2))
        m["st_in"] = np.ascontiguousarray(stg[:, sb])
        m["cc_in"] = np.ascontiguousarray(cc[:, sb].reshape(cc.shape[0], NSB, 3, 24, 128).transpose(0, 1, 4, 3, 2))
        m["ck_in"] = np.ascontiguousarray(ck[:, sb].transpose(0, 1, 3, 4, 2))
        m["cv_in"] = np.ascontiguousarray(cv[:, sb])
        in_maps.append(m)
    res = run_bass_kernel_spmd(nc, in_maps, core_ids=list(range(n_cores)))
    return res.results


def kernel(**inputs):
    B, T, _ = inputs["x_prompt"].shape
    SBT, TS, _ = inputs["x_sample"].shape
    PAST = inputs["cache_k"].shape[2]
    n_cores = 8
    NSB = SBT // n_cores
    prompt_of_core = [c % B for c in range(n_cores)]
    r = run(inputs, n_cores, T, PAST, NSB, TS, prompt_of_core)
    y_p = np.stack([r[b]["y_p"] for b in range(B)])
    y_s = np.concatenate([r[c]["y_s"] for c in range(n_cores)], axis=0)
    st_p = np.stack([r[b]["st_p"] for b in range(B)], axis=1)
    conv_p = np.stack([r[b]["conv_p"] for b in range(B)], axis=1)
    k_p = np.stack([r[b]["k_p"] for b in range(B)], axis=1).reshape(2, B, T, H, 128)
    v_p = np.stack([r[b]["v_p"] for b in range(B)], axis=1).reshape(2, B, T, H, 128)
    st_s = np.concatenate([r[c]["st_s"] for c in range(n_cores)], axis=1)
    conv_s = np.concatenate([r[c]["conv_s"] for c in range(n_cores)], axis=1)
    k_s = np.concatenate([r[c]["k_s"] for c in range(n_cores)], axis=1).reshape(2, SBT, TS, H, 128)
    v_s = np.concatenate([r[c]["v_s"] for c in range(n_cores)], axis=1).reshape(2, SBT, TS, H, 128)
    return tuple(np.ascontiguousarray(a.astype(np.float32)) for a in
                 (y_p, y_s, st_p, conv_p, k_p, v_p, st_s, conv_s, k_s, v_s))
```
